# Optimizing a Trainium2 kernel written in Bass

```python
import math
import jax, jax.numpy as jnp
from jax import lax
import numpy as np

D_MODEL = 2048
BATCH = 8
SEQ = 2048
DEPTH = 1
DEC_BATCH = 8
DEC_SEQ = 32
PAST_LEN = 2048

CHUNK = 64
D_MIX = D_MODEL
CONV_DIM = D_MIX // 2
CONV_WIDTH = 31
CONV_STATE = CONV_WIDTH - 1
SB_HEADS = 8
SB_HEAD_DIM = (D_MIX - CONV_DIM) // SB_HEADS
SB_DIM = SB_HEADS * SB_HEAD_DIM
SB_BLOCK = 128
MEM_TOKENS = 256
XATTN_HEADS = 4
XATTN_DIM = D_MODEL // 2
XATTN_HEAD_DIM = XATTN_DIM // XATTN_HEADS
D_FF = ((8 * D_MODEL // 3) + 255) // 256 * 256
IN_COLS = 2 * CONV_DIM + 3 * SB_DIM
LN_EPS = 1e-5
DEEPNORM_ALPHA = (2.0 * DEPTH) ** 0.25
DEEPNORM_BETA = (8.0 * DEPTH) ** -0.25

kernel_name = 'hybrid_conformer_stickbreak_stream_step'


def layer_norm(x, g, b):
    xf = x.astype(jnp.float32)
    mu = jnp.mean(xf, axis=-1, keepdims=True)
    var = jnp.mean(jnp.square(xf - mu), axis=-1, keepdims=True)
    y = (xf - mu) * lax.rsqrt(var + LN_EPS) * g.astype(jnp.float32) + b.astype(jnp.float32)
    return y.astype(x.dtype)


def post_norm(x, sub, g, b):
    return layer_norm(DEEPNORM_ALPHA * x + sub, g, b)


def swiglu_ffn(x, w_gate, w_up, w_down):
    return (jax.nn.silu(x @ w_gate) * (x @ w_up)) @ w_down


def conv_module(u, prev, conv_w, conv_b, ln_g, ln_b):
    a, gate = jnp.split(u, 2, axis=-1)
    h = a * jax.nn.sigmoid(gate)
    hp = jnp.concatenate([prev.astype(h.dtype), h], axis=1)
    y = lax.conv_general_dilated(
        hp, conv_w[:, None, :].astype(h.dtype), window_strides=(1,), padding='VALID',
        dimension_numbers=('NWC', 'WIO', 'NWC'), feature_group_count=CONV_DIM) + conv_b
    y = jax.nn.silu(layer_norm(y, ln_g, ln_b))
    return y, hp[:, -CONV_STATE:]


def sb_block(q, k, v, q_pos, k_pos):
    z = jnp.einsum('bqhd,bkhd->bhqk', q.astype(jnp.float32), k.astype(jnp.float32)) * (SB_HEAD_DIM ** -0.5)
    mask = k_pos[None, :] < q_pos[:, None]
    log_stay = jnp.where(mask, jax.nn.log_sigmoid(-z), 0.0)
    after = lax.cumsum(log_stay, axis=3, reverse=True) - log_stay
    a = jnp.where(mask, jnp.exp(jax.nn.log_sigmoid(z) + after), 0.0)
    return jnp.einsum('bhqk,bkhd->bqhd', a, v.astype(jnp.float32)).astype(q.dtype)


def sb_attention(q, k, v, q_pos, k_pos):
    b, t = q.shape[0], q.shape[1]
    if t <= SB_BLOCK:
        return sb_block(q, k, v, q_pos, k_pos)
    nb = t // SB_BLOCK
    qb = jnp.moveaxis(q.reshape(b, nb, SB_BLOCK, SB_HEADS, SB_HEAD_DIM), 1, 0)
    pb = q_pos.reshape(nb, SB_BLOCK)
    ob = lax.map(lambda qp: sb_block(qp[0], k, v, qp[1], k_pos), (qb, pb))
    return jnp.moveaxis(ob, 0, 1).reshape(b, t, SB_HEADS, SB_HEAD_DIM)


def memory_kv(mem, wk, wv):
    b, m, _ = mem.shape
    k = (mem @ wk).reshape(b, m, XATTN_HEADS, XATTN_HEAD_DIM)
    v = (mem @ wv).reshape(b, m, XATTN_HEADS, XATTN_HEAD_DIM)
    return k, v


def cross_attention(x, mem_k, mem_v, wq, wo):
    b, t, _ = x.shape
    q = (x @ wq).reshape(b, t, XATTN_HEADS, XATTN_HEAD_DIM)
    s = jnp.einsum('bqhd,bmhd->bhqm', q.astype(jnp.float32), mem_k.astype(jnp.float32)) * (XATTN_HEAD_DIM ** -0.5)
    p = jax.nn.softmax(s, axis=-1)
    o = jnp.einsum('bhqm,bmhd->bqhd', p, mem_v.astype(jnp.float32)).reshape(b, t, XATTN_DIM).astype(x.dtype)
    return o @ wo


def encoder_layer(x, conv_prev, k_past, v_past, mem_k, mem_v, f1g, f1u, f1d, ln1g, ln1b, w_in, conv_w, conv_b,
                  cln_g, cln_b, w_out, ln2g, ln2b, xq, xo, ln3g, ln3b, f2g, f2u, f2d, ln4g, ln4b):
    b, t, _ = x.shape
    p_len = k_past.shape[1]
    x = post_norm(x, 0.5 * swiglu_ffn(x, f1g, f1u, f1d), ln1g, ln1b)
    proj = x @ w_in
    u_conv, q, k, v = jnp.split(proj, [2 * CONV_DIM, 2 * CONV_DIM + SB_DIM, 2 * CONV_DIM + 2 * SB_DIM], axis=-1)
    y_conv, conv_new = conv_module(u_conv, conv_prev, conv_w, conv_b, cln_g, cln_b)
    q = q.reshape(b, t, SB_HEADS, SB_HEAD_DIM)
    k = k.reshape(b, t, SB_HEADS, SB_HEAD_DIM)
    v = v.reshape(b, t, SB_HEADS, SB_HEAD_DIM)
    k_all = jnp.concatenate([k_past.astype(k.dtype), k], axis=1)
    v_all = jnp.concatenate([v_past.astype(v.dtype), v], axis=1)
    q_pos = p_len + jnp.arange(t, dtype=jnp.int32)
    k_pos = jnp.arange(p_len + t, dtype=jnp.int32)
    y_sb = sb_attention(q, k_all, v_all, q_pos, k_pos).reshape(b, t, SB_DIM)
    mix = jnp.concatenate([y_conv, y_sb], axis=-1) @ w_out
    x = post_norm(x, mix, ln2g, ln2b)
    x = post_norm(x, cross_attention(x, mem_k, mem_v, xq, xo), ln3g, ln3b)
    x = post_norm(x, 0.5 * swiglu_ffn(x, f2g, f2u, f2d), ln4g, ln4b)
    return x, conv_new, k, v


def setup_inputs(seed: int = 0) -> dict:
    key = jax.random.key(seed)
    ks = iter(jax.random.split(key, 48))

    def nrm(shape, scale=1.0):
        return jax.random.normal(next(ks), shape, jnp.float32) * scale

    def dense(fan_in, fan_out, scale=1.0):
        return nrm((DEPTH, fan_in, fan_out), scale * fan_in ** -0.5)

    def gain(n):
        return 1.0 + nrm((DEPTH, n), 0.02)

    def bias(n):
        return nrm((DEPTH, n), 0.02)

    in_scale = jnp.concatenate([jnp.ones((2 * CONV_DIM + 2 * SB_DIM,), jnp.float32),
                                jnp.full((SB_DIM,), DEEPNORM_BETA, jnp.float32)])
    return {
        'x_prompt': nrm((BATCH, SEQ, D_MODEL)),
        'x_sample': nrm((DEC_BATCH, DEC_SEQ, D_MODEL)),
        'mem_prompt': nrm((BATCH, MEM_TOKENS, D_MODEL)),
        'cache_conv': nrm((DEPTH, DEC_BATCH, CONV_STATE, CONV_DIM), 0.5),
        'cache_sb_k': nrm((DEPTH, DEC_BATCH, PAST_LEN, SB_HEADS, SB_HEAD_DIM)),
        'cache_sb_v': nrm((DEPTH, DEC_BATCH, PAST_LEN, SB_HEADS, SB_HEAD_DIM), DEEPNORM_BETA),
        'cache_mem_k': nrm((DEPTH, DEC_BATCH, MEM_TOKENS, XATTN_HEADS, XATTN_HEAD_DIM)),
        'cache_mem_v': nrm((DEPTH, DEC_BATCH, MEM_TOKENS, XATTN_HEADS, XATTN_HEAD_DIM), DEEPNORM_BETA),
        'ffn1_w_gate': dense(D_MODEL, D_FF),
        'ffn1_w_up': dense(D_MODEL, D_FF, DEEPNORM_BETA),
        'ffn1_w_down': dense(D_FF, D_MODEL, DEEPNORM_BETA),
        'ln1_g': gain(D_MODEL),
        'ln1_b': bias(D_MODEL),
        'w_in': dense(D_MODEL, IN_COLS) * in_scale,
        'conv_w': nrm((DEPTH, CONV_WIDTH, CONV_DIM), CONV_WIDTH ** -0.5),
        'conv_b': bias(CONV_DIM),
        'conv_ln_g': gain(CONV_DIM),
        'conv_ln_b': bias(CONV_DIM),
        'w_out': dense(D_MIX, D_MODEL, DEEPNORM_BETA),
        'ln2_g': gain(D_MODEL),
        'ln2_b': bias(D_MODEL),
        'xattn_wq': dense(D_MODEL, XATTN_DIM),
        'xattn_wk': dense(D_MODEL, XATTN_DIM),
        'xattn_wv': dense(D_MODEL, XATTN_DIM, DEEPNORM_BETA),
        'xattn_wo': dense(XATTN_DIM, D_MODEL, DEEPNORM_BETA),
        'ln3_g': gain(D_MODEL),
        'ln3_b': bias(D_MODEL),
        'ffn2_w_gate': dense(D_MODEL, D_FF),
        'ffn2_w_up': dense(D_MODEL, D_FF, DEEPNORM_BETA),
        'ffn2_w_down': dense(D_FF, D_MODEL, DEEPNORM_BETA),
        'ln4_g': gain(D_MODEL),
        'ln4_b': bias(D_MODEL),
    }


def reference(x_prompt, x_sample, mem_prompt, cache_conv, cache_sb_k, cache_sb_v, cache_mem_k, cache_mem_v,
              ffn1_w_gate, ffn1_w_up, ffn1_w_down, ln1_g, ln1_b, w_in, conv_w, conv_b, conv_ln_g, conv_ln_b,
              w_out, ln2_g, ln2_b, xattn_wq, xattn_wk, xattn_wv, xattn_wo, ln3_g, ln3_b,
              ffn2_w_gate, ffn2_w_up, ffn2_w_down, ln4_g, ln4_b):
    xp, xs = x_prompt, x_sample
    b_p = x_prompt.shape[0]
    conv_p, k_p, v_p, mk_p, mv_p = [], [], [], [], []
    conv_s, k_s, v_s = [], [], []
    for l in range(DEPTH):
        w = (ffn1_w_gate[l], ffn1_w_up[l], ffn1_w_down[l], ln1_g[l], ln1_b[l], w_in[l], conv_w[l], conv_b[l],
             conv_ln_g[l], conv_ln_b[l], w_out[l], ln2_g[l], ln2_b[l], xattn_wq[l], xattn_wo[l], ln3_g[l], ln3_b[l],
             ffn2_w_gate[l], ffn2_w_up[l], ffn2_w_down[l], ln4_g[l], ln4_b[l])
        mk, mv = memory_kv(mem_prompt, xattn_wk[l], xattn_wv[l])
        zero_conv = jnp.zeros((b_p, CONV_STATE, CONV_DIM), xp.dtype)
        zero_kv = jnp.zeros((b_p, 0, SB_HEADS, SB_HEAD_DIM), xp.dtype)
        xp, cp, kp, vp = encoder_layer(xp, zero_conv, zero_kv, zero_kv, mk, mv, *w)
        conv_p.append(cp); k_p.append(kp); v_p.append(vp); mk_p.append(mk); mv_p.append(mv)
        xs, cs, ks_, vs_ = encoder_layer(xs, cache_conv[l], cache_sb_k[l], cache_sb_v[l],
                                         cache_mem_k[l], cache_mem_v[l], *w)
        conv_s.append(cs); k_s.append(ks_); v_s.append(vs_)
    return (xp, xs, jnp.stack(conv_p), jnp.stack(k_p), jnp.stack(v_p), jnp.stack(mk_p), jnp.stack(mv_p),
            jnp.stack(conv_s), jnp.stack(k_s), jnp.stack(v_s))
```

```python
import math
from contextlib import ExitStack

import numpy as np
import concourse.bass as bass
import concourse.mybir as mybir
from concourse.bass_utils import run_bass_kernel_spmd

F32 = mybir.dt.float32
BF16 = mybir.dt.bfloat16
AF = mybir.ActivationFunctionType
ALU = mybir.AluOpType
AX = mybir.AxisListType

D = 2048
NCH = 16
CONV_DIM = 1024
CW = 31
CS = 30
SBH = 8
XH = 4
MEM = 256
DEC = 32
LN_EPS = 1e-5
ALPHA = 2.0 ** 0.25
NPAR = 8 * 16 + 8 * CW + 24
NSB = 21
NW = 3


class Buf:
    __slots__ = ("name", "w", "r")

    def __init__(self, name):
        self.name = name
        self.w = None
        self.r = {}


class Sched:
    def __init__(self, nc, es):
        self.nc = nc
        self.es = es
        self.eng = {"pe": nc.tensor, "act": nc.scalar, "dve": nc.vector, "pool": nc.gpsimd, "sp": nc.sync}
        self.sem = {e: es.enter_context(nc.semaphore("s_" + e)) for e in ("pe", "act", "dve", "pool")}
        self.cnt = {e: 0 for e in self.sem}
        self.seen = {e: {} for e in self.eng}
        self.dsems = []
        self.out_tags = {}

    def new_dsem(self, name):
        h = self.es.enter_context(self.nc.semaphore(name))
        self.dsems.append([h, 0])
        return len(self.dsems) - 1

    def _wait(self, e, tag):
        kind, key, val = tag
        if kind == "e" and key == e and e == "pe":
            return
        k = (kind, key)
        if self.seen[e].get(k, 0) >= val:
            return
        self.seen[e][k] = val
        sem = self.sem[key] if kind == "e" else self.dsems[key][0]
        self.eng[e].wait_ge(sem, val)

    def _deps(self, e, reads, writes):
        for b in reads:
            if b.w is not None:
                self._wait(e, b.w)
        for b in writes:
            if b.w is not None:
                self._wait(e, b.w)
            for k, v in b.r.items():
                self._wait(e, (k[0], k[1], v))

    def _mark(self, tag, reads, writes):
        k = (tag[0], tag[1])
        for b in writes:
            b.w = tag
            b.r = {}
        for b in reads:
            if b not in writes:
                if b.r.get(k, 0) < tag[2]:
                    b.r[k] = tag[2]

    def op(self, e, fn, reads=(), writes=(), signal=True):
        if any(b.name.startswith("PS") for b in reads):
            writes = list(writes) + [b for b in reads if b.name.startswith("PS")]
            reads = [b for b in reads if not b.name.startswith("PS")]
        self._deps(e, reads, writes)
        ins = fn(self.eng[e])
        if signal:
            self.cnt[e] += 1
            ins.then_inc(self.sem[e], 1)
            tag = ("e", e, self.cnt[e])
        else:
            tag = ("e", e, self.cnt[e] + 1)
        self._mark(tag, reads, writes)
        return ins

    def dma(self, q, out, in_, dsem, reads=(), writes=(), is_out=False):
        self._deps(q, reads, writes)
        ins = self.eng[q].dma_start(out=out, in_=in_)
        self.dsems[dsem][1] += 16
        ins.then_inc(self.dsems[dsem][0], 16)
        tag = ("d", dsem, self.dsems[dsem][1])
        self._mark(tag, reads, writes)
        if is_out:
            self.out_tags[dsem] = tag
        return ins

    def finish(self):
        for tag in self.out_tags.values():
            self._wait("sp", tag)


def build_program(SEQ, PAST, DFF, TT):
    FC = DFF // 128
    NTILES = SEQ // TT
    LK = max(SEQ, PAST + DEC)
    NVB = (LK + 127) // 128
    nc = bass.Bass("TRN2", target_bir_lowering=False)

    def din(name, shape):
        return nc.dram_tensor(name, list(shape), F32, kind="ExternalInput").ap()

    def dout(name, shape):
        return nc.dram_tensor(name, list(shape), F32, kind="ExternalOutput").ap()

    x_p = din("x_p", [SEQ, D]); x_s = din("x_s", [DEC, D]); mem_p = din("mem_p", [MEM, D])
    c_conv = din("c_conv", [CS, CONV_DIM]); c_k = din("c_k", [PAST, 1024]); c_v = din("c_v", [PAST, 1024])
    c_mk = din("c_mk", [MEM, 1024]); c_mv = din("c_mv", [MEM, 1024])
    w1g = din("w1g", [D, DFF]); w1u = din("w1u", [D, DFF]); w1d = din("w1d", [DFF, D])
    w_in = din("w_in", [D, 5120]); w_out = din("w_out", [D, D])
    wq = din("wq", [D, 1024]); wk = din("wk", [D, 1024]); wv = din("wv", [D, 1024]); wo = din("wo", [1024, D])
    w2g = din("w2g", [D, DFF]); w2u = din("w2u", [D, DFF]); w2d = din("w2d", [DFF, D])
    params = din("params", [128, NPAR]); cident = din("cident", [128, 128]); cmask = din("cmask", [128, 128])

    y_p = dout("y_p", [SEQ, D]); y_s = dout("y_s", [DEC, D]); conv_p = dout("conv_p", [CS, CONV_DIM])
    k_p = dout("k_p", [SEQ, 1024]); v_p = dout("v_p", [SEQ, 1024])
    mk_p = dout("mk_p", [MEM, 1024]); mv_p = dout("mv_p", [MEM, 1024])
    conv_s = dout("conv_s", [CS, CONV_DIM]); k_s = dout("k_s", [DEC, 1024]); v_s = dout("v_s", [DEC, 1024])

    es = ExitStack()
    with es:
        S = Sched(nc, es)

        def sb(name, shape, dt):
            return es.enter_context(nc.sbuf_tensor(name, list(shape), dt))

        R_t = sb("R", [128, NCH, TT], F32); R = R_t[:]
        XT_t = sb("XT", [128, NCH, TT], BF16); XT = XT_t[:]
        KT_t = sb("KT", [128, SBH, LK], BF16); KT = KT_t[:]
        VV_t = sb("VV", [128, NVB, 1024], BF16); VV = VV_t[:]
        WS = [sb("WS%d" % i, [128, 8, 512], BF16)[:] for i in range(NW)]
        STGA = sb("STGA", [128, 1024], F32)[:]
        STGB = sb("STGB", [128, 1024], F32)[:]
        HP = sb("HP", [128, 2, 544], F32)[:]
        SCR = [sb("SCR%d" % i, [128, 512], F32)[:] for i in range(NSB)]
        SCRB = [a.bitcast(BF16) for a in SCR]
        identF = sb("identF", [128, 128], F32)[:]
        identB = sb("identB", [128, 128], BF16)[:]
        onesF = sb("onesF", [128, 128], F32)[:]
        onesB = sb("onesB", [128, 512], BF16)[:]
        maskF = sb("maskF", [128, 128], F32)[:]
        PAR = sb("PAR", [128, NPAR], F32)[:]
        APAR = sb("APAR", [128, 128], F32)[:]
        CST = sb("CST", [128, 8, 32], F32)[:]
        MKT = sb("MKT", [128, 8, MEM], BF16)[:]
        MV = sb("MV", [128, 2, 1024], BF16)[:]
        CELL = sb("CELL", [128, 64], F32)[:]

        PS = [es.enter_context(nc.psum_tensor("PS%d" % i, [128, 512], F32))[:] for i in range(8)]
        PSB16 = [a.bitcast(BF16) for a in PS]

        bR = [Buf("R%d" % c) for c in range(NCH)]
        bXT = [Buf("XT%d" % c) for c in range(NCH)]
        bKT = Buf("KT"); bVV = Buf("VV"); bVV16 = Buf("VV16")
        bWS = [Buf("WS%d" % i) for i in range(NW)]
        dWS = [S.new_dsem("dws%d" % i) for i in range(NW)]
        bSTGA = Buf("STGA"); bSTGB = Buf("STGB")
        dSTGA = S.new_dsem("dstga"); dSTGB = S.new_dsem("dstgb")
        dOUTA = S.new_dsem("douta"); dOUTB = S.new_dsem("doutb")
        dCONST = S.new_dsem("dconst"); dVC = S.new_dsem("dvc"); dMC = S.new_dsem("dmc")
        bHP = [Buf("HP0"), Buf("HP1")]
        bSCR = [Buf("SCR%d" % i) for i in range(NSB)]
        bCONST = Buf("CONST"); bCST = Buf("CST"); bMKT = Buf("MKT"); bMV = Buf("MV")
        bCELL = [Buf("CELL%d" % i) for i in range(64)]
        bPS = [Buf("PS%d" % i) for i in range(8)]

        st = {"bank": 0, "w": 0, "cell": 0, "stg": 0}
        dSCR = [S.new_dsem("dscr%d" % i) for i in range(NSB)]

        def next_stg(ring):
            i = ring[st["stg"] % len(ring)]
            st["stg"] += 1
            return i

        RING_IO = list(range(0, 16))
        RING_KV = list(range(0, 8))

        def next_bank(exclude=None):
            i = st["bank"]
            if exclude is not None and i == exclude:
                i = (i + 1) % 8
            st["bank"] = (i + 1) % 8
            return i

        def next_cell():
            i = st["cell"]
            st["cell"] = (i + 1) % 64
            return i

        S.dma("sp", PAR, params, dCONST, writes=[bCONST])
        S.dma("sp", identF, cident, dCONST, writes=[bCONST])
        S.dma("sp", maskF, cmask, dCONST, writes=[bCONST])
        S.op("dve", lambda e: e.memset(onesF, 1.0), writes=[bCONST])
        S.op("dve", lambda e: e.memset(onesB, 1.0), writes=[bCONST])
        S.op("dve", lambda e: e.tensor_copy(out=identB, in_=identF), reads=[bCONST], writes=[bCONST])
        S.op("dve", lambda e: e.tensor_scalar(out=APAR, in0=PAR[:, 0:128], scalar1=ALPHA, scalar2=None, op0=ALU.mult),
             reads=[bCONST], writes=[bCONST])

        def par_ln(i):
            return (PAR[:, 32 * i:32 * i + 16], PAR[:, 32 * i + 16:32 * i + 32],
                    APAR[:, 32 * i:32 * i + 16], APAR[:, 32 * i + 16:32 * i + 32])

        CWP = PAR[:, 128:128 + 8 * CW]
        CBP = PAR[:, 128 + 8 * CW:128 + 8 * CW + 8]
        CGP = PAR[:, 128 + 8 * CW + 8:128 + 8 * CW + 16]
        CLB = PAR[:, 128 + 8 * CW + 16:128 + 8 * CW + 24]

        NLMAX = 176
        wcache = nc.dram_tensor("wcache", [NLMAX, 128, 8, 512], BF16, kind="Internal").ap()
        bWC = [Buf("WC%d" % i) for i in range(NLMAX)]
        dWB = [S.new_dsem("dwb%d" % i) for i in range(NW)]
        dWSH = [S.new_dsem("dwsh%d" % i) for i in range(NW)]
        st["wmode"] = "direct"
        st["li"] = 0

        def load_w(W, k0, kn, segs):
            i = st["w"] % NW
            st["w"] += 1
            if st["wmode"] == "cached":
                li = st["li"]
                st["li"] += 1
                S.dma("sp", WS[i], wcache[li], dWSH[i], reads=[bWC[li]], writes=[bWS[i]])
                return i
            off = 0
            for (col, wd) in segs:
                src = W[k0 * 128:(k0 + kn) * 128, col:col + wd].rearrange("(k p) n -> p k n", p=128)
                S.dma("pool", WS[i][:, 0:kn, off:off + wd], src, dWS[i], writes=[bWS[i]])
                off += wd
            if st["wmode"] == "fill":
                li = st["li"]
                st["li"] += 1
                assert li < NLMAX
                S.dma("sp", wcache[li], WS[i], dWB[i], reads=[bWS[i]], writes=[bWC[li]])
            return i

        def linear_fm(W, KC, rhs_fn, rhs_bufs_fn, groups, NT, epilogue):
            nks = (KC + 7) // 8
            for gi, segs in enumerate(groups):
                width = sum(w for _, w in segs)
                nchk = width // 128
                banks = [next_bank() for _ in range(nchk)]
                for ks in range(nks):
                    kn = min(8, KC - ks * 8)
                    si = load_w(W, ks * 8, kn, segs)
                    for j in range(nchk):
                        for k in range(kn):
                            kc = ks * 8 + k
                            last = kc == KC - 1
                            S.op("pe", lambda e, j=j, k=k, kc=kc, last=last, si=si: e.matmul(
                                out=PS[banks[j]][:, 0:NT], lhsT=WS[si][:, k, j * 128:(j + 1) * 128],
                                rhs=rhs_fn(kc), start=(kc == 0), stop=last),
                                 reads=[bWS[si]] + rhs_bufs_fn(kc), writes=[bPS[banks[j]]], signal=last)
                epilogue(gi, banks)

        def linear_tm(W, KC, lhs_fn, lhs_bufs_fn, cols, NT, epilogue):
            TB = min(NT, 128)
            NBLK = (NT + 127) // 128
            nks = (KC + 7) // 8
            for gi, col in enumerate(cols):
                banks = [next_bank() for _ in range(NBLK)]
                for ks in range(nks):
                    kn = min(8, KC - ks * 8)
                    si = load_w(W, ks * 8, kn, [(col, 512)])
                    for b in range(NBLK):
                        for k in range(kn):
                            kc = ks * 8 + k
                            last = kc == KC - 1
                            S.op("pe", lambda e, b=b, k=k, kc=kc, last=last, si=si: e.matmul(
                                out=PS[banks[b]][0:TB, 0:512], lhsT=lhs_fn(kc, b),
                                rhs=WS[si][:, k, 0:512], start=(kc == 0), stop=last),
                                 reads=[bWS[si]] + lhs_bufs_fn(kc), writes=[bPS[banks[b]]], signal=last)
                epilogue(gi, banks)

        def tm_to_fm(src, src_bufs, nrows, ncols, evac):
            nchk = ncols // 128
            for g in range((nchk + 3) // 4):
                bk = next_bank()
                nj = min(4, nchk - g * 4)
                for j in range(nj):
                    c = g * 4 + j
                    S.op("pe", lambda e, c=c, j=j, bk=bk: e.matmul(
                        out=PS[bk][:, j * 128:j * 128 + nrows], lhsT=src[0:nrows, c * 128:(c + 1) * 128],
                        rhs=identF[0:nrows, 0:nrows], start=True, stop=True),
                         reads=src_bufs + [bCONST], writes=[bPS[bk]])
                evac(g, bk, nj)

        def v3(ap2d, nj, n):
            return ap2d[:, 0:nj * 128].rearrange("p (j q) -> p j q", q=128)[:, :, 0:n]

        LNB = 16

        def layer_norm(src_fn, src_bufs_fn, nchk, NT, out_fn, tb=None):
            tb = tb or [LNB, LNB + 1, LNB + 2, LNB + 3, LNB + 4]
            SQ = [SCR[tb[0]], SCR[tb[1]]]; bSQ = [bSCR[tb[0]], bSCR[tb[1]]]
            TMP = SQ; bTMP = bSQ
            MEAN = SCR[tb[2]]; bMEAN = bSCR[tb[2]]
            RSTD = SCR[tb[3]]; bRSTD = bSCR[tb[3]]
            NMR = SCR[tb[4]]; bNMR = bSCR[tb[4]]
            b1 = next_bank(); b2 = next_bank()
            dim = float(nchk * 128)
            for c in range(nchk):
                last = c == nchk - 1
                S.op("pe", lambda e, c=c, last=last: e.matmul(out=PS[b1][:, 0:NT], lhsT=onesF, rhs=src_fn(c),
                                                               start=(c == 0), stop=last),
                     reads=[bCONST] + src_bufs_fn(c), writes=[bPS[b1]])
                S.op("act", lambda e, c=c: e.activation(out=SQ[c % 2][:, 0:NT], in_=src_fn(c), func=AF.Square),
                     reads=src_bufs_fn(c), writes=[bSQ[c % 2]])
                S.op("pe", lambda e, c=c, last=last: e.matmul(out=PS[b2][:, 0:NT], lhsT=onesF,
                                                               rhs=SQ[c % 2][:, 0:NT],
                                                               start=(c == 0), stop=last),
                     reads=[bCONST, bSQ[c % 2]], writes=[bPS[b2]])
            S.op("act", lambda e: e.activation(out=MEAN[:, 0:NT], in_=PS[b1][:, 0:NT], func=AF.Copy, scale=1.0 / dim),
                 reads=[bPS[b1]], writes=[bMEAN])
            S.op("dve", lambda e: e.tensor_tensor(out=RSTD[:, 0:NT], in0=MEAN[:, 0:NT], in1=MEAN[:, 0:NT], op=ALU.mult),
                 reads=[bMEAN], writes=[bRSTD])
            S.op("dve", lambda e: e.scalar_tensor_tensor(out=RSTD[:, 0:NT], in0=PS[b2][:, 0:NT], scalar=1.0 / dim,
                                                         in1=RSTD[:, 0:NT], op0=ALU.mult, op1=ALU.subtract),
                 reads=[bPS[b2]], writes=[bRSTD])
            S.op("act", lambda e: e.activation(out=RSTD[:, 0:NT], in_=RSTD[:, 0:NT], func=AF.Ln, bias=LN_EPS, scale=1.0),
                 writes=[bRSTD])
            S.op("act", lambda e: e.activation(out=RSTD[:, 0:NT], in_=RSTD[:, 0:NT], func=AF.Exp, scale=-0.5),
                 writes=[bRSTD])
            S.op("dve", lambda e: e.scalar_tensor_tensor(out=NMR[:, 0:NT], in0=MEAN[:, 0:NT], scalar=-1.0,
                                                         in1=RSTD[:, 0:NT], op0=ALU.mult, op1=ALU.mult),
                 reads=[bMEAN, bRSTD], writes=[bNMR])
            for c in range(nchk):
                t = TMP[c % 2][:, 0:NT]
                S.op("dve", lambda e, c=c, t=t: e.tensor_tensor(out=t, in0=src_fn(c), in1=RSTD[:, 0:NT], op=ALU.mult),
                     reads=src_bufs_fn(c) + [bRSTD], writes=[bTMP[c % 2]])
                S.op("dve", lambda e, t=t: e.tensor_tensor(out=t, in0=t, in1=NMR[:, 0:NT], op=ALU.add),
                     reads=[bNMR], writes=[bTMP[c % 2]])
                out_fn(c, t, bTMP[c % 2])

        def ln_to_stream(i, NT):
            g, b, ag, ab = par_ln(i)

            def out_fn(c, t, bt):
                S.op("act", lambda e: e.activation(out=R[:, c, 0:NT], in_=t, func=AF.Identity,
                                                   scale=ag[:, c:c + 1], bias=ab[:, c:c + 1]),
                     reads=[bt, bCONST], writes=[bR[c]])
                S.op("act", lambda e: e.activation(out=XT[:, c, 0:NT], in_=t, func=AF.Identity,
                                                   scale=g[:, c:c + 1], bias=b[:, c:c + 1]),
                     reads=[bt, bCONST], writes=[bXT[c]])

            layer_norm(lambda c: R[:, c, 0:NT], lambda c: [bR[c]], NCH, NT, out_fn)

        def xt_rhs(NT):
            return (lambda kc: XT[:, kc, 0:NT]), (lambda kc: [bXT[kc]])

        def ffn(Wg, Wu, Wd, NT):
            rhs_fn, rhs_bufs = xt_rhs(NT)
            fgs = []
            f = 0
            while f < FC:
                n = min(8, FC - f)
                fgs.append((f, n))
                f += n
            for fi, (f0, fn_) in enumerate(fgs):
                hb = (fi % 2) * 4
                gb = 8 + (fi % 2) * 4

                def HTc(j):
                    return SCRB[hb + j // 2][:, (j % 2) * 512:(j % 2) * 512 + NT]

                def bHTc(j):
                    return bSCR[hb + j // 2]

                sub = []
                j0 = 0
                while j0 < fn_:
                    sub.append((j0, min(4, fn_ - j0)))
                    j0 += 4
                for (j0, jn) in sub:
                    col = (f0 + j0) * 128

                    def ep_gate(gi, banks, j0=j0, jn=jn):
                        for j in range(jn):
                            S.op("act", lambda e, j=j: e.activation(out=SCR[gb + j][:, 0:NT], in_=PS[banks[j]][:, 0:NT],
                                                                     func=AF.Silu),
                                 reads=[bPS[banks[j]]], writes=[bSCR[gb + j]])

                    def ep_up(gi, banks, j0=j0, jn=jn):
                        for j in range(jn):
                            S.op("dve", lambda e, j=j: e.tensor_tensor(out=HTc(j0 + j), in0=PS[banks[j]][:, 0:NT],
                                                                        in1=SCR[gb + j][:, 0:NT], op=ALU.mult),
                                 reads=[bPS[banks[j]], bSCR[gb + j]], writes=[bHTc(j0 + j)])

                    linear_fm(Wg, NCH, rhs_fn, rhs_bufs, [[(col, jn * 128)]], NT, ep_gate)
                    linear_fm(Wu, NCH, rhs_fn, rhs_bufs, [[(col, jn * 128)]], NT, ep_up)

                def ep_down(gi, banks):
                    for j in range(4):
                        c = gi * 4 + j
                        S.op("dve", lambda e, j=j, c=c: e.scalar_tensor_tensor(
                            out=R[:, c, 0:NT], in0=PS[banks[j]][:, 0:NT], scalar=0.5, in1=R[:, c, 0:NT],
                            op0=ALU.mult, op1=ALU.add),
                             reads=[bPS[banks[j]]], writes=[bR[c]])

                Wd_g = Wd[f0 * 128:(f0 + fn_) * 128, :]
                linear_fm(Wd_g, fn_, lambda kc: HTc(kc), lambda kc: [bHTc(kc)],
                          [[(g * 512, 512)] for g in range(4)], NT, ep_down)

        def load_x(x_ap, t0, NT):
            TB = min(NT, 128)
            for b in range((NT + 127) // 128):
                for g in range(4):
                    si = next_stg(RING_IO)
                    S.dma("sp", SCR[si][0:TB, :], x_ap[t0 + b * 128:t0 + b * 128 + TB, g * 512:(g + 1) * 512], dSCR[si],
                          writes=[bSCR[si]])
                    bk = next_bank()
                    for j in range(4):
                        S.op("pe", lambda e, j=j: e.matmul(
                            out=PS[bk][:, j * 128:j * 128 + TB], lhsT=SCR[si][0:TB, j * 128:(j + 1) * 128],
                            rhs=identF[0:TB, 0:TB], start=True, stop=True),
                             reads=[bSCR[si], bCONST], writes=[bPS[bk]])
                    c0 = g * 4
                    S.op("act", lambda e: e.activation(out=R[:, c0:c0 + 4, b * 128:b * 128 + TB],
                                                       in_=v3(PS[bk], 4, TB), func=AF.Copy, scale=ALPHA),
                         reads=[bPS[bk]], writes=[bR[c0 + j] for j in range(4)])
                    S.op("dve", lambda e: e.tensor_copy(out=XT[:, c0:c0 + 4, b * 128:b * 128 + TB],
                                                        in_=v3(PS[bk], 4, TB)),
                         reads=[bPS[bk]], writes=[bXT[c0 + j] for j in range(4)])

        def store_fm(src_fn, src_bufs_fn, nchk, NT, out_ap, t0):
            TB = min(NT, 128)
            k = 0
            for b in range((NT + 127) // 128):
                for g in range(nchk // 4):
                    bk = next_bank()
                    for j in range(4):
                        c = g * 4 + j
                        S.op("pe", lambda e, c=c, j=j: e.matmul(
                            out=PS[bk][0:TB, j * 128:(j + 1) * 128], lhsT=src_fn(c)[:, b * 128:b * 128 + TB],
                            rhs=identF, start=True, stop=True),
                             reads=src_bufs_fn(c) + [bCONST], writes=[bPS[bk]])
                    si = next_stg(RING_IO)
                    if k % 2 == 0:
                        S.op("act", lambda e: e.activation(out=SCR[si][0:TB, :], in_=PS[bk][0:TB, :], func=AF.Copy),
                             reads=[bPS[bk]], writes=[bSCR[si]])
                    else:
                        S.op("dve", lambda e: e.tensor_copy(out=SCR[si][0:TB, :], in_=PS[bk][0:TB, :]),
                             reads=[bPS[bk]], writes=[bSCR[si]])
                    k += 1
                    S.dma("sp", out_ap[t0 + b * 128:t0 + b * 128 + TB, g * 512:(g + 1) * 512], SCR[si][0:TB, :], dSCR[si],
                          reads=[bSCR[si]], is_out=True)

        def store_conv_state(out_ap):
            for g in range(2):
                bk = next_bank()
                for j in range(4):
                    c = g * 4 + j
                    S.op("pe", lambda e, c=c, j=j, bk=bk: e.matmul(
                        out=PS[bk][0:CS, j * 128:(j + 1) * 128], lhsT=CST[:, c, 0:CS], rhs=identF, start=True, stop=True),
                         reads=[bCST, bCONST], writes=[bPS[bk]])
                S.op("act", lambda e, g=g, bk=bk: e.activation(out=STGB[0:CS, g * 512:(g + 1) * 512],
                                                               in_=PS[bk][0:CS, 0:512], func=AF.Copy),
                     reads=[bPS[bk]], writes=[bSTGB])
            S.dma("sp", out_ap[:, :], STGB[0:CS, :], dOUTB, reads=[bSTGB], is_out=True)

        def load_conv_cache():
            S.dma("sp", STGB[0:CS, :], c_conv[:, :], dSTGB, writes=[bSTGB])
            bk = next_bank()
            for c in range(8):
                S.op("pe", lambda e, c=c: e.matmul(out=PS[bk][:, c * 32:c * 32 + CS],
                                                    lhsT=STGB[0:CS, c * 128:(c + 1) * 128],
                                                    rhs=identF[0:CS, 0:CS], start=True, stop=True),
                     reads=[bSTGB, bCONST], writes=[bPS[bk]])
            S.op("act", lambda e: e.activation(
                out=CST[:, :, 0:CS], in_=PS[bk][:, 0:256].rearrange("p (c t) -> p c t", t=32)[:, :, 0:CS], func=AF.Copy),
                 reads=[bPS[bk]], writes=[bCST])

        def k_rows_to_KT(src, src_bufs, nrows, pos):
            def evac(g, bk, nj):
                S.op("dve", lambda e: e.tensor_copy(out=KT[:, g * 4:g * 4 + nj, pos:pos + nrows], in_=v3(PS[bk], nj, nrows)),
                     reads=[bPS[bk]], writes=[bKT])
            tm_to_fm(src, src_bufs, nrows, 1024, evac)

        def load_kv_cache():
            S.dma("pool", VV[:, 0:PAST // 128, :], c_v.rearrange("(b p) n -> p b n", p=128), dVC, writes=[bVV])
            for blk in range(PAST // 128):
                S.dma("sp", STGB[:, :], c_k[blk * 128:(blk + 1) * 128, :], dSTGB, writes=[bSTGB])
                k_rows_to_KT(STGB, [bSTGB], 128, blk * 128)

        def mem_rows_to_MKT(src, src_bufs, blk):
            def evac(g, bk, nj):
                S.op("dve", lambda e: e.tensor_copy(out=MKT[:, g * 4:g * 4 + nj, blk * 128:(blk + 1) * 128],
                                                    in_=v3(PS[bk], nj, 128)),
                     reads=[bPS[bk]], writes=[bMKT])
            tm_to_fm(src, src_bufs, 128, 1024, evac)

        def load_mem_cache():
            S.dma("pool", MV[:, :, :], c_mv.rearrange("(b p) n -> p b n", p=128), dMC, writes=[bMV])
            for blk in range(2):
                S.dma("sp", STGB[:, :], c_mk[blk * 128:(blk + 1) * 128, :], dSTGB, writes=[bSTGB])
                mem_rows_to_MKT(STGB, [bSTGB], blk)

        def prompt_mem_kv():
            def memT(c):
                return SCRB[c // 4][:, (c % 4) * 256:(c % 4) * 256 + MEM]

            for b in range(2):
              for hh in range(2):
                S.dma("sp", STGA[:, :], mem_p[b * 128:(b + 1) * 128, hh * 1024:(hh + 1) * 1024], dSTGA, writes=[bSTGA])

                def evac(g, bk, nj, b=b, hh=hh):
                    gq = hh * 2 + g
                    dst = SCRB[gq][:, 0:1024].rearrange("p (j q) -> p j q", q=256)[:, 0:nj, b * 128:(b + 1) * 128]
                    S.op("dve", lambda e: e.tensor_copy(out=dst, in_=v3(PS[bk], nj, 128)),
                         reads=[bPS[bk]], writes=[bSCR[gq]])
                tm_to_fm(STGA, [bSTGA], 128, 1024, evac)

            for (W, out_ap, is_k) in ((wk, mk_p, True), (wv, mv_p, False)):
                def ep(gi, banks, out_ap=out_ap, is_k=is_k):
                    for b in range(2):
                        S.op("act", lambda e, b=b: e.activation(out=STGB[:, 0:512], in_=PS[banks[b]][:, 0:512], func=AF.Copy),
                             reads=[bPS[banks[b]]], writes=[bSTGB])
                        S.dma("sp", out_ap[b * 128:(b + 1) * 128, gi * 512:(gi + 1) * 512], STGB[:, 0:512], dOUTB,
                              reads=[bSTGB], is_out=True)
                        if is_k:
                            def evac(g, bk, nj, b=b):
                                S.op("dve", lambda e: e.tensor_copy(
                                    out=MKT[:, gi * 4:gi * 4 + nj, b * 128:(b + 1) * 128], in_=v3(PS[bk], nj, 128)),
                                     reads=[bPS[bk]], writes=[bMKT])
                            tm_to_fm(STGB, [bSTGB], 128, 512, evac)
                        else:
                            S.op("dve", lambda e, b=b: e.tensor_copy(out=MV[:, b, gi * 512:(gi + 1) * 512],
                                                                     in_=PS[banks[b]][:, 0:512]),
                                 reads=[bPS[banks[b]]], writes=[bMV])
                linear_tm(W, NCH, lambda kc, b: memT(kc)[:, b * 128:(b + 1) * 128], lambda kc: [bSCR[kc // 4]],
                          [0, 512], MEM, ep)

        QTB = 16

        def QTh(h, NT):
            return SCRB[QTB + h // 2][:, (h % 2) * 512:(h % 2) * 512 + NT]

        def proj_qkv(NT, pos0, k_out, v_out, t0):
            TB = min(NT, 128)
            NBLK = (NT + 127) // 128
            rhs_fn, rhs_bufs = xt_rhs(NT)

            def ep_q(gi, banks):
                for j in range(4):
                    h = gi * 4 + j
                    S.op("act", lambda e, j=j, h=h: e.activation(out=QTh(h, NT), in_=PS[banks[j]][:, 0:NT], func=AF.Copy),
                         reads=[bPS[banks[j]]], writes=[bSCR[QTB + h // 2]])
            linear_fm(w_in, NCH, rhs_fn, rhs_bufs, [[(2048, 512)], [(2560, 512)]], NT, ep_q)

            lhs_fn = lambda kc, b: XT[:, kc, b * 128:b * 128 + TB]
            lhs_bufs = lambda kc: [bXT[kc]]

            def ep_k(gi, banks):
                for b in range(NBLK):
                    si = next_stg(RING_KV)
                    S.op("act", lambda e, b=b: e.activation(out=SCR[si][0:TB, 0:512], in_=PS[banks[b]][0:TB, 0:512], func=AF.Copy),
                         reads=[bPS[banks[b]]], writes=[bSCR[si]])
                    S.dma("sp", k_out[t0 + b * 128:t0 + b * 128 + TB, gi * 512:(gi + 1) * 512], SCR[si][0:TB, 0:512], dSCR[si],
                          reads=[bSCR[si]], is_out=True)

                    def evac(g, bk, nj, b=b):
                        S.op("act", lambda e: e.activation(
                            out=KT[:, gi * 4:gi * 4 + nj, pos0 + b * 128:pos0 + b * 128 + TB], in_=v3(PS[bk], nj, TB),
                            func=AF.Copy),
                             reads=[bPS[bk]], writes=[bKT])
                    tm_to_fm(SCR[si], [bSCR[si]], TB, 512, evac)
            linear_tm(w_in, NCH, lhs_fn, lhs_bufs, [3072, 3584], NT, ep_k)

            def ep_v(gi, banks):
                for b in range(NBLK):
                    si = next_stg(RING_KV)
                    S.op("act", lambda e, b=b: e.activation(out=SCR[si][0:TB, 0:512], in_=PS[banks[b]][0:TB, 0:512], func=AF.Copy),
                         reads=[bPS[banks[b]]], writes=[bSCR[si]])
                    S.dma("sp", v_out[t0 + b * 128:t0 + b * 128 + TB, gi * 512:(gi + 1) * 512], SCR[si][0:TB, 0:512], dSCR[si],
                          reads=[bSCR[si]], is_out=True)
                    S.op("act", lambda e, b=b: e.activation(
                        out=VV[0:TB, (pos0 + b * 128) // 128, gi * 512:(gi + 1) * 512], in_=PS[banks[b]][0:TB, 0:512],
                        func=AF.Copy),
                         reads=[bPS[banks[b]]], writes=[bVV, bVV16])
            linear_tm(w_in, NCH, lhs_fn, lhs_bufs, [4096, 4608], NT, ep_v)

        def MIXc(c, NT):
            return SCRB[c // 2][:, (c % 2) * 512:(c % 2) * 512 + NT]

        def emit_pipelined(chains):
            if not chains:
                return
            nst = max(len(c) for c in chains)
            for step in range(len(chains) + nst - 1):
                for sg in range(nst - 1, -1, -1):
                    i = step - sg
                    if 0 <= i < len(chains) and sg < len(chains[i]):
                        chains[i][sg]()

        pools = {"s": [0, 1, 2], "t": [3, 4], "o": [5, 6, 7]}
        pool_i = {"s": 0, "t": 0, "o": 0}

        def pool_bank(p):
            b = pools[p][pool_i[p] % len(pools[p])]
            pool_i[p] += 1
            return b

        def sb_attention(NT, pos0):
            TB = min(NT, 128)
            scale = 128 ** -0.5
            use_pool = st["wmode"] == "cached"
            sets = []
            for base in (8, 12):
                sets.append(dict(E=SCR[base], SP=SCR[base + 1], PN=SCR[base + 2], AB=SCRB[base + 3],
                                 bufs=[bSCR[base + i] for i in range(4)]))
            sets.append(dict(E=STGA[:, 0:512], SP=STGA[:, 512:1024], PN=HP[:, 0, 0:512], AB=SCRB[20],
                             bufs=[bSTGA, bHP[0], bSCR[20]]))
            if NT == TT and NVB * 128 > SEQ:
                sets.append(dict(E=STGB[:, 0:512], SP=STGB[:, 512:1024], PN=HP[:, 1, 0:512], AB=VV[:, NVB - 1, :],
                                 bufs=[bSTGB, bHP[1], bVV16]))
            NSETS = len(sets)
            for st_ in sets:
                S.op("dve", lambda e, st_=st_: e.memset(st_["PN"][:, 0:1], 0.0), writes=st_["bufs"])
            chains = []
            ci = 0
            for qb in range((NT + 127) // 128):
                g0 = pos0 + qb * 128
                if TB == 128:
                    s1 = max(0, g0 + TB - 512)
                    chunks = [(s1, g0 + TB - s1, True)]
                    end = s1
                else:
                    chunks = [(g0, TB, True)]
                    end = g0
                while end > 0:
                    s0 = max(0, end - 512)
                    chunks.append((s0, end - s0, False))
                    end = s0
                nblk_total = sum((n + 127) // 128 for (_, n, _) in chunks)
                for h in range(SBH):
                    hs = dict(ob=None, prev_nb=None, blk_i=0)
                    for ck, (s, n, diag) in enumerate(chunks):
                        cs = dict(set=sets[ci % NSETS])
                        ci += 1
                        last_chunk = ck == len(chunks) - 1

                        def mk(qb=qb, h=h, s=s, n=n, diag=diag, hs=hs, cs=cs, last_chunk=last_chunk,
                               nblk_total=nblk_total):
                            st_ = cs["set"]
                            bset = st_["bufs"]
                            E = st_["E"][0:TB, 0:n]
                            SP = st_["SP"][0:TB, 0:n]
                            PN = st_["PN"]
                            A = st_["AB"][0:TB, 0:n]
                            AT = st_["AB"][:, 512:1024]
                            qT = QTh(h, NT)[:, qb * 128:qb * 128 + TB]
                            bq = bSCR[QTB + h // 2]
                            nkb = (n + 127) // 128

                            def st0():
                                cs["sbk"] = sbk = pool_bank("s")
                                S.op("pe", lambda e: e.matmul(out=PS[sbk][0:TB, 0:n], lhsT=qT, rhs=KT[:, h, s:s + n],
                                                              start=True, stop=True),
                                     reads=[bq, bKT], writes=[bPS[sbk]])

                            def st1():
                                sbk = cs["sbk"]
                                S.op("act", lambda e: e.activation(out=E, in_=PS[sbk][0:TB, 0:n], func=AF.Exp, scale=scale),
                                     reads=[bPS[sbk]], writes=bset)
                                if diag:
                                    Ed = st_["E"][0:TB, n - TB:n]
                                    S.op("dve", lambda e: e.tensor_tensor(out=Ed, in0=Ed, in1=maskF[0:TB, 0:TB], op=ALU.mult),
                                         reads=[bCONST], writes=bset)
                                cs["ct"] = ct = next_cell()
                                S.op("act", lambda e: e.activation(out=SP, in_=E, func=AF.Ln, bias=1.0, scale=1.0,
                                                                   accum_out=CELL[0:TB, ct:ct + 1]),
                                     writes=bset + [bCELL[ct]])

                            def st2():
                                ct = cs["ct"]
                                if n > 1:
                                    S.op("dve", lambda e: e.tensor_tensor_scan(
                                        out=PN[0:TB, 1:n], data0=onesB[0:TB, 0:n - 1], data1=st_["SP"][0:TB, 0:n - 1],
                                        initial=0.0, op0=ALU.mult, op1=ALU.add),
                                         reads=[bCONST], writes=bset)
                                cs["cn"] = cn = next_cell()
                                if hs["prev_nb"] is None:
                                    S.op("dve", lambda e: e.tensor_scalar(out=CELL[0:TB, cn:cn + 1], in0=CELL[0:TB, ct:ct + 1],
                                                                          scalar1=-1.0, scalar2=None, op0=ALU.mult),
                                         reads=[bCELL[ct]], writes=[bCELL[cn]])
                                else:
                                    pn_ = hs["prev_nb"]
                                    S.op("dve", lambda e: e.tensor_tensor(out=CELL[0:TB, cn:cn + 1],
                                                                          in0=CELL[0:TB, pn_:pn_ + 1],
                                                                          in1=CELL[0:TB, ct:ct + 1], op=ALU.subtract),
                                         reads=[bCELL[ct], bCELL[pn_]], writes=[bCELL[cn]])
                                hs["prev_nb"] = cn

                            def st3():
                                cn = cs["cn"]
                                S.op("act", lambda e: e.activation(out=SP, in_=PN[0:TB, 0:n], func=AF.Exp, scale=1.0,
                                                                   bias=CELL[0:TB, cn:cn + 1]),
                                     reads=[bCELL[cn]], writes=bset)
                                S.op("pool" if use_pool else "dve",
                                     lambda e: e.tensor_tensor(out=A, in0=E, in1=SP, op=ALU.mult), writes=bset)

                            def st4():
                                cs["tbk"] = tbk = pool_bank("t")
                                for kb in range(nkb):
                                    kn = min(128, n - kb * 128)
                                    S.op("pe", lambda e, kb=kb, kn=kn: e.transpose(
                                        out=PSB16[tbk][0:kn, kb * 128:kb * 128 + TB],
                                        in_=st_["AB"][0:TB, kb * 128:kb * 128 + kn], identity=identB[0:TB, 0:TB]),
                                         reads=bset + [bCONST], writes=[bPS[tbk]])
                                kn0 = min(128, n)
                                if use_pool:
                                    S.op("dve", lambda e: e.tensor_copy(
                                        out=AT[0:kn0, 0:nkb * 128].rearrange("p (k q) -> p k q", q=128)[:, :, 0:TB],
                                        in_=PSB16[tbk][0:kn0, 0:nkb * 128].rearrange("p (k q) -> p k q", q=128)[:, :, 0:TB]),
                                         reads=[bPS[tbk]], writes=bset)
                                else:
                                    S.op("act", lambda e: e.activation(
                                        out=AT[0:kn0, 0:nkb * 128].rearrange("p (k q) -> p k q", q=128)[:, :, 0:TB],
                                        in_=PSB16[tbk][0:kn0, 0:nkb * 128].rearrange("p (k q) -> p k q", q=128)[:, :, 0:TB],
                                        func=AF.Copy),
                                         reads=[bPS[tbk]], writes=bset)

                            def st5():
                                if hs["ob"] is None:
                                    hs["ob"] = pool_bank("o")
                                ob = hs["ob"]
                                for kb in range(nkb):
                                    kn = min(128, n - kb * 128)
                                    first = hs["blk_i"] == 0
                                    last = hs["blk_i"] == nblk_total - 1
                                    S.op("pe", lambda e, kb=kb, kn=kn, first=first, last=last: e.matmul(
                                        out=PS[ob][:, 0:TB], lhsT=VV[0:kn, (s + kb * 128) // 128, h * 128:(h + 1) * 128],
                                        rhs=AT[0:kn, kb * 128:kb * 128 + TB], start=first, stop=last),
                                         reads=bset + [bVV], writes=[bPS[ob]])
                                    hs["blk_i"] += 1
                                if last_chunk:
                                    S.op("act", lambda e: e.activation(out=MIXc(8 + h, NT)[:, qb * 128:qb * 128 + TB],
                                                                       in_=PS[ob][:, 0:TB], func=AF.Copy),
                                         reads=[bPS[ob]], writes=[bSCR[(8 + h) // 2]])

                            return [st0, st1, st2, st3, st4, st5]

                        chains.append(mk())
            emit_pipelined(chains)

        def conv_module(NT, first_tile):
            rhs_fn, rhs_bufs = xt_rhs(NT)
            YB = 8
            SIGB = 16

            def ep_pair(gi, banks):
                for i in range(2):
                    c = gi * 2 + i
                    hp = HP[:, i, :]
                    if first_tile:
                        S.op("dve", lambda e, hp=hp: e.memset(hp[:, 0:CS], 0.0), writes=[bHP[i]])
                    else:
                        S.op("act", lambda e, hp=hp, c=c: e.activation(out=hp[:, 0:CS], in_=CST[:, c, 0:CS], func=AF.Copy),
                             reads=[bCST], writes=[bHP[i]])
                    S.op("act", lambda e, i=i: e.activation(out=SCR[SIGB + i][:, 0:NT], in_=PS[banks[2 + i]][:, 0:NT],
                                                             func=AF.Sigmoid),
                         reads=[bPS[banks[2 + i]]], writes=[bSCR[SIGB + i]])
                    S.op("dve", lambda e, i=i, hp=hp: e.tensor_tensor(out=hp[:, CS:CS + NT], in0=PS[banks[i]][:, 0:NT],
                                                                       in1=SCR[SIGB + i][:, 0:NT], op=ALU.mult),
                         reads=[bPS[banks[i]], bSCR[SIGB + i]], writes=[bHP[i]])
                    S.op("act", lambda e, hp=hp, c=c: e.activation(out=CST[:, c, 0:CS], in_=hp[:, NT:NT + CS], func=AF.Copy),
                         reads=[bHP[i]], writes=[bCST])
                for j in range(CW):
                    for i in range(2):
                        c = gi * 2 + i
                        hp = HP[:, i, :]
                        y = SCR[YB + c][:, 0:NT]
                        wcol = CWP[:, c * CW + j:c * CW + j + 1]
                        teng = "dve"
                        if j == 0:
                            S.op(teng, lambda e, hp=hp, y=y, wcol=wcol, c=c: e.tensor_scalar(
                                out=y, in0=hp[:, 0:NT], scalar1=wcol, scalar2=CBP[:, c:c + 1], op0=ALU.mult, op1=ALU.add),
                                 reads=[bHP[i], bCONST], writes=[bSCR[YB + c]])
                        else:
                            S.op(teng, lambda e, hp=hp, y=y, wcol=wcol, j=j: e.scalar_tensor_tensor(
                                out=y, in0=hp[:, j:j + NT], scalar=wcol, in1=y, op0=ALU.mult, op1=ALU.add),
                                 reads=[bHP[i], bCONST], writes=[bSCR[YB + c]])

            groups = [[(c0 * 128, 256), (1024 + c0 * 128, 256)] for c0 in range(0, 8, 2)]
            linear_fm(w_in, NCH, rhs_fn, rhs_bufs, groups, NT, ep_pair)

        def conv_norm(NT):
            YB = 8

            def out_fn(c, t, bt):
                S.op("act", lambda e: e.activation(out=MIXc(c, NT), in_=t, func=AF.Silu,
                                                   scale=CGP[:, c:c + 1], bias=CLB[:, c:c + 1]),
                     reads=[bt, bCONST], writes=[bSCR[c // 2]])
            layer_norm(lambda c: SCR[YB + c][:, 0:NT], lambda c: [bSCR[YB + c]], 8, NT, out_fn, tb=[4, 5, 6, 7, 20])

        def cross_attention(NT):
            TB = min(NT, 128)
            rhs_fn, rhs_bufs = xt_rhs(NT)
            scale = 256 ** -0.5

            def QX(c):
                return SCRB[c // 2][:, (c % 2) * 512:(c % 2) * 512 + NT]

            def OX(c):
                return SCRB[4 + c // 2][:, (c % 2) * 512:(c % 2) * 512 + NT]

            def ep_q(gi, banks):
                for j in range(4):
                    c = gi * 4 + j
                    S.op("act", lambda e, j=j, c=c: e.activation(out=QX(c), in_=PS[banks[j]][:, 0:NT], func=AF.Copy),
                         reads=[bPS[banks[j]]], writes=[bSCR[c // 2]])
            linear_fm(wq, NCH, rhs_fn, rhs_bufs, [[(0, 512)], [(512, 512)]], NT, ep_q)

            chains = []
            it = 0
            for qb in range((NT + 127) // 128):
                for h in range(XH):
                    blk = 8 + it % 6
                    it += 1

                    def mk(qb=qb, h=h, blk=blk):
                        cs = {}
                        bs = [bSCR[blk]]
                        E = SCR[blk][0:TB, 0:MEM]
                        P = SCRB[blk][0:TB, 512:512 + MEM]
                        PT = SCRB[blk][:, 768:1024]

                        def st0():
                            cs["sbk"] = sbk = pool_bank("s")
                            for i in range(2):
                                S.op("pe", lambda e, i=i: e.matmul(
                                    out=PS[sbk][0:TB, 0:MEM], lhsT=QX(2 * h + i)[:, qb * 128:qb * 128 + TB],
                                    rhs=MKT[:, 2 * h + i, :], start=(i == 0), stop=(i == 1)),
                                     reads=[bSCR[(2 * h + i) // 2], bMKT], writes=[bPS[sbk]], signal=(i == 1))

                        def st1():
                            sbk = cs["sbk"]
                            cs["cm"] = cm = next_cell()
                            cs["cs"] = cs_ = next_cell()
                            S.op("dve", lambda e: e.reduce_max(out=CELL[0:TB, cm:cm + 1], in_=PS[sbk][0:TB, 0:MEM], axis=AX.X),
                                 reads=[bPS[sbk]], writes=[bCELL[cm]])
                            S.op("dve", lambda e: e.tensor_scalar(out=CELL[0:TB, cm:cm + 1], in0=CELL[0:TB, cm:cm + 1],
                                                                  scalar1=-scale, scalar2=None, op0=ALU.mult),
                                 writes=[bCELL[cm]])
                            S.op("act", lambda e: e.activation(out=E, in_=PS[sbk][0:TB, 0:MEM], func=AF.Exp, scale=scale,
                                                               bias=CELL[0:TB, cm:cm + 1],
                                                               accum_out=CELL[0:TB, cs_:cs_ + 1]),
                                 reads=[bPS[sbk], bCELL[cm]], writes=bs + [bCELL[cs_]])

                        def st2():
                            cs_ = cs["cs"]
                            S.op("dve", lambda e: e.reciprocal(out=CELL[0:TB, cs_:cs_ + 1], in_=CELL[0:TB, cs_:cs_ + 1]),
                                 writes=[bCELL[cs_]])
                            S.op("dve", lambda e: e.tensor_scalar(out=P, in0=E, scalar1=CELL[0:TB, cs_:cs_ + 1],
                                                                  scalar2=None, op0=ALU.mult),
                                 reads=[bCELL[cs_]], writes=bs)

                        def st3():
                            cs["tbk"] = tbk = pool_bank("t")
                            for m in range(2):
                                S.op("pe", lambda e, m=m: e.transpose(
                                    out=PSB16[tbk][:, m * 128:m * 128 + TB],
                                    in_=SCRB[blk][0:TB, 512 + m * 128:512 + (m + 1) * 128], identity=identB[0:TB, 0:TB]),
                                     reads=bs + [bCONST], writes=[bPS[tbk]])
                            S.op("act", lambda e: e.activation(
                                out=PT.rearrange("p (k q) -> p k q", q=128)[:, :, 0:TB],
                                in_=PSB16[tbk][:, 0:256].rearrange("p (k q) -> p k q", q=128)[:, :, 0:TB], func=AF.Copy),
                                 reads=[bPS[tbk]], writes=bs)

                        def st4():
                            ob = pool_bank("o")
                            for i in range(2):
                                for m in range(2):
                                    S.op("pe", lambda e, i=i, m=m: e.matmul(
                                        out=PS[ob][:, i * 128:i * 128 + TB],
                                        lhsT=MV[:, m, h * 256 + i * 128:h * 256 + (i + 1) * 128],
                                        rhs=PT[:, m * 128:m * 128 + TB], start=(m == 0), stop=(m == 1)),
                                         reads=bs + [bMV], writes=[bPS[ob]], signal=(m == 1))
                            for i in range(2):
                                S.op("dve", lambda e, i=i: e.tensor_copy(out=OX(2 * h + i)[:, qb * 128:qb * 128 + TB],
                                                                         in_=PS[ob][:, i * 128:i * 128 + TB]),
                                     reads=[bPS[ob]], writes=[bSCR[4 + (2 * h + i) // 2]])

                        return [st0, st1, st2, st3, st4]

                    chains.append(mk())
            emit_pipelined(chains)

            def ep_o(gi, banks):
                for j in range(4):
                    c = gi * 4 + j
                    S.op("dve", lambda e, j=j, c=c: e.tensor_tensor(out=R[:, c, 0:NT], in0=PS[banks[j]][:, 0:NT],
                                                                     in1=R[:, c, 0:NT], op=ALU.add),
                         reads=[bPS[banks[j]]], writes=[bR[c]])
            linear_fm(wo, 8, lambda kc: OX(kc), lambda kc: [bSCR[4 + kc // 2]],
                      [[(g * 512, 512)] for g in range(4)], NT, ep_o)

        def run_tile(x_ap, t0, NT, pos0, k_out, v_out, y_out, first_tile, conv_out):
            rhs_fn, rhs_bufs = xt_rhs(NT)
            st["li"] = 0
            st["wmode"] = "fill" if (first_tile and NT == TT) else "cached"
            load_x(x_ap, t0, NT)
            ffn(w1g, w1u, w1d, NT)
            ln_to_stream(0, NT)
            conv_module(NT, first_tile)
            proj_qkv(NT, pos0, k_out, v_out, t0)
            if conv_out is not None:
                store_conv_state(conv_out)
            conv_norm(NT)
            sb_attention(NT, pos0)

            def ep_out(gi, banks):
                for j in range(4):
                    c = gi * 4 + j
                    S.op("dve", lambda e, j=j, c=c: e.tensor_tensor(out=R[:, c, 0:NT], in0=PS[banks[j]][:, 0:NT],
                                                                     in1=R[:, c, 0:NT], op=ALU.add),
                         reads=[bPS[banks[j]]], writes=[bR[c]])
            linear_fm(w_out, NCH, lambda kc: MIXc(kc, NT), lambda kc: [bSCR[kc // 2]],
                      [[(g * 512, 512)] for g in range(4)], NT, ep_out)
            ln_to_stream(1, NT)
            cross_attention(NT)
            ln_to_stream(2, NT)
            ffn(w2g, w2u, w2d, NT)
            g4, b4, _, _ = par_ln(3)

            def out_fn(c, t, bt):
                S.op("act", lambda e: e.activation(out=R[:, c, 0:NT], in_=t, func=AF.Identity,
                                                   scale=g4[:, c:c + 1], bias=b4[:, c:c + 1]),
                     reads=[bt, bCONST], writes=[bR[c]])
            layer_norm(lambda c: R[:, c, 0:NT], lambda c: [bR[c]], NCH, NT, out_fn)
            store_fm(lambda c: R[:, c, :], lambda c: [bR[c]], NCH, NT, y_out, t0)

        prompt_mem_kv()
        for ti in range(NTILES):
            run_tile(x_p, ti * TT, TT, ti * TT, k_p, v_p, y_p, ti == 0, conv_p if ti == NTILES - 1 else None)
        load_kv_cache()
        load_mem_cache()
        load_conv_cache()
        run_tile(x_s, 0, DEC, PAST, k_s, v_s, y_s, False, conv_s)
        S.finish()
        info = dict(cnt=dict(S.cnt), nw=st["w"])
    return nc, info


_PROG = {}


def _pack_params(inp):
    def fm(v):
        v = np.asarray(v, np.float32).reshape(-1, 128)
        return np.ascontiguousarray(v.T)
    cols = []
    for g_, b_ in ((inp["ln1_g"], inp["ln1_b"]), (inp["ln2_g"], inp["ln2_b"]),
                   (inp["ln3_g"], inp["ln3_b"]), (inp["ln4_g"], inp["ln4_b"])):
        cols.append(fm(g_[0]))
        cols.append(fm(b_[0]))
    cw = np.asarray(inp["conv_w"][0], np.float32)
    cwp = cw.reshape(CW, 8, 128).transpose(2, 1, 0).reshape(128, 8 * CW)
    cols.append(np.ascontiguousarray(cwp))
    cols.append(fm(inp["conv_b"][0]))
    cols.append(fm(inp["conv_ln_g"][0]))
    cols.append(fm(inp["conv_ln_b"][0]))
    return np.ascontiguousarray(np.concatenate(cols, axis=1).astype(np.float32))


def run(inp, SEQ, PAST, DFF, TT, trace=False):
    key = (SEQ, PAST, DFF, TT)
    if key not in _PROG:
        _PROG[key] = build_program(*key)
    nc, info = _PROG[key]
    n = 8
    f = lambda a: np.ascontiguousarray(np.asarray(a, np.float32))
    shared = {
        "w1g": f(inp["ffn1_w_gate"][0]), "w1u": f(inp["ffn1_w_up"][0]), "w1d": f(inp["ffn1_w_down"][0]),
        "w_in": f(inp["w_in"][0]), "w_out": f(inp["w_out"][0]),
        "wq": f(inp["xattn_wq"][0]), "wk": f(inp["xattn_wk"][0]), "wv": f(inp["xattn_wv"][0]),
        "wo": f(inp["xattn_wo"][0]),
        "w2g": f(inp["ffn2_w_gate"][0]), "w2u": f(inp["ffn2_w_up"][0]), "w2d": f(inp["ffn2_w_down"][0]),
        "params": _pack_params(inp),
        "cident": np.eye(128, dtype=np.float32),
        "cmask": np.tril(np.ones((128, 128), np.float32), -1),
    }
    in_maps = []
    for i in range(n):
        m = dict(shared)
        m["x_p"] = f(inp["x_prompt"][i]); m["x_s"] = f(inp["x_sample"][i]); m["mem_p"] = f(inp["mem_prompt"][i])
        m["c_conv"] = f(inp["cache_conv"][0, i])
        m["c_k"] = f(inp["cache_sb_k"][0, i]).reshape(PAST, 1024)
        m["c_v"] = f(inp["cache_sb_v"][0, i]).reshape(PAST, 1024)
        m["c_mk"] = f(inp["cache_mem_k"][0, i]).reshape(MEM, 1024)
        m["c_mv"] = f(inp["cache_mem_v"][0, i]).reshape(MEM, 1024)
        in_maps.append(m)
    res = run_bass_kernel_spmd(nc, in_maps, core_ids=list(range(n)), trace=trace)
    r = res.results

    def stack(name, shape):
        return np.stack([np.asarray(r[i][name], np.float32).reshape(shape) for i in range(n)], axis=0)

    outs = (
        stack("y_p", (SEQ, D)), stack("y_s", (DEC, D)),
        stack("conv_p", (CS, CONV_DIM))[None],
        stack("k_p", (SEQ, SBH, 128))[None], stack("v_p", (SEQ, SBH, 128))[None],
        stack("mk_p", (MEM, XH, 256))[None], stack("mv_p", (MEM, XH, 256))[None],
        stack("conv_s", (CS, CONV_DIM))[None],
        stack("k_s", (DEC, SBH, 128))[None], stack("v_s", (DEC, SBH, 128))[None],
    )
    return outs, res


def kernel(**inputs):
    outs, _ = run(inputs, 2048, 2048, 5632, 512)
    return outs
```

```python
import math
from contextlib import ExitStack

import numpy as np
import concourse.bass as bass
import concourse.mybir as mybir
from concourse.bass_utils import run_bass_kernel_spmd

F32 = mybir.dt.float32
BF16 = mybir.dt.bfloat16
AF = mybir.ActivationFunctionType
ALU = mybir.AluOpType
AX = mybir.AxisListType

D = 2048
NCH = 16
CONV_DIM = 1024
CW = 31
CS = 30
SBH = 8
XH = 4
MEM = 256
DEC = 32
LN_EPS = 1e-5
ALPHA = 2.0 ** 0.25
NPAR = 8 * 16 + 8 * CW + 24
NSB = 21
NW = 3


class Buf:
    __slots__ = ("name", "w", "r")

    def __init__(self, name):
        self.name = name
        self.w = None
        self.r = {}


class Sched:
    def __init__(self, nc, es):
        self.nc = nc
        self.es = es
        self.eng = {"pe": nc.tensor, "act": nc.scalar, "dve": nc.vector, "pool": nc.gpsimd, "sp": nc.sync}
        self.sem = {e: es.enter_context(nc.semaphore("s_" + e)) for e in ("pe", "act", "dve", "pool")}
        self.cnt = {e: 0 for e in self.sem}
        self.seen = {e: {} for e in self.eng}
        self.dsems = []
        self.out_tags = {}

    def new_dsem(self, name):
        h = self.es.enter_context(self.nc.semaphore(name))
        self.dsems.append([h, 0])
        return len(self.dsems) - 1

    def _wait(self, e, tag):
        kind, key, val = tag
        if kind == "e" and key == e and e == "pe":
            return
        k = (kind, key)
        if self.seen[e].get(k, 0) >= val:
            return
        self.seen[e][k] = val
        sem = self.sem[key] if kind == "e" else self.dsems[key][0]
        self.eng[e].wait_ge(sem, val)

    def _deps(self, e, reads, writes):
        for b in reads:
            if b.w is not None:
                self._wait(e, b.w)
        for b in writes:
            if b.w is not None:
                self._wait(e, b.w)
            for k, v in b.r.items():
                self._wait(e, (k[0], k[1], v))

    def _mark(self, tag, reads, writes):
        k = (tag[0], tag[1])
        for b in writes:
            b.w = tag
            b.r = {}
        for b in reads:
            if b not in writes:
                if b.r.get(k, 0) < tag[2]:
                    b.r[k] = tag[2]

    def op(self, e, fn, reads=(), writes=(), signal=True):
        if any(b.name.startswith("PS") for b in reads):
            writes = list(writes) + [b for b in reads if b.name.startswith("PS")]
            reads = [b for b in reads if not b.name.startswith("PS")]
        self._deps(e, reads, writes)
        ins = fn(self.eng[e])
        if signal:
            self.cnt[e] += 1
            ins.then_inc(self.sem[e], 1)
            tag = ("e", e, self.cnt[e])
        else:
            tag = ("e", e, self.cnt[e] + 1)
        self._mark(tag, reads, writes)
        return ins

    def dma(self, q, out, in_, dsem, reads=(), writes=(), is_out=False):
        self._deps(q, reads, writes)
        ins = self.eng[q].dma_start(out=out, in_=in_)
        self.dsems[dsem][1] += 16
        ins.then_inc(self.dsems[dsem][0], 16)
        tag = ("d", dsem, self.dsems[dsem][1])
        self._mark(tag, reads, writes)
        if is_out:
            self.out_tags[dsem] = tag
        return ins

    def finish(self):
        for tag in self.out_tags.values():
            self._wait("sp", tag)


def build_program(SEQ, PAST, DFF, TT):
    FC = DFF // 128
    NTILES = SEQ // TT
    LK = max(SEQ, PAST + DEC)
    NVB = (LK + 127) // 128
    nc = bass.Bass("TRN2", target_bir_lowering=False)

    def din(name, shape):
        return nc.dram_tensor(name, list(shape), F32, kind="ExternalInput").ap()

    def dout(name, shape):
        return nc.dram_tensor(name, list(shape), F32, kind="ExternalOutput").ap()

    x_p = din("x_p", [SEQ, D]); x_s = din("x_s", [DEC, D]); mem_p = din("mem_p", [MEM, D])
    c_conv = din("c_conv", [CS, CONV_DIM]); c_k = din("c_k", [PAST, 1024]); c_v = din("c_v", [PAST, 1024])
    c_mk = din("c_mk", [MEM, 1024]); c_mv = din("c_mv", [MEM, 1024])
    w1g = din("w1g", [D, DFF]); w1u = din("w1u", [D, DFF]); w1d = din("w1d", [DFF, D])
    w_in = din("w_in", [D, 5120]); w_out = din("w_out", [D, D])
    wq = din("wq", [D, 1024]); wk = din("wk", [D, 1024]); wv = din("wv", [D, 1024]); wo = din("wo", [1024, D])
    w2g = din("w2g", [D, DFF]); w2u = din("w2u", [D, DFF]); w2d = din("w2d", [DFF, D])
    params = din("params", [128, NPAR]); cident = din("cident", [128, 128]); cmask = din("cmask", [128, 128])

    y_p = dout("y_p", [SEQ, D]); y_s = dout("y_s", [DEC, D]); conv_p = dout("conv_p", [CS, CONV_DIM])
    k_p = dout("k_p", [SEQ, 1024]); v_p = dout("v_p", [SEQ, 1024])
    mk_p = dout("mk_p", [MEM, 1024]); mv_p = dout("mv_p", [MEM, 1024])
    conv_s = dout("conv_s", [CS, CONV_DIM]); k_s = dout("k_s", [DEC, 1024]); v_s = dout("v_s", [DEC, 1024])

    es = ExitStack()
    with es:
        S = Sched(nc, es)

        def sb(name, shape, dt):
            return es.enter_context(nc.sbuf_tensor(name, list(shape), dt))

        R_t = sb("R", [128, NCH, TT], F32); R = R_t[:]
        XT_t = sb("XT", [128, NCH, TT], BF16); XT = XT_t[:]
        KT_t = sb("KT", [128, SBH, LK], BF16); KT = KT_t[:]
        VV_t = sb("VV", [128, NVB, 1024], BF16); VV = VV_t[:]
        WS = [sb("WS%d" % i, [128, 8, 512], BF16)[:] for i in range(NW)]
        STGA = sb("STGA", [128, 1024], F32)[:]
        STGB = sb("STGB", [128, 1024], F32)[:]
        HP = sb("HP", [128, 2, 544], F32)[:]
        SCR = [sb("SCR%d" % i, [128, 512], F32)[:] for i in range(NSB)]
        SCRB = [a.bitcast(BF16) for a in SCR]
        identF = sb("identF", [128, 128], F32)[:]
        identB = sb("identB", [128, 128], BF16)[:]
        onesF = sb("onesF", [128, 128], F32)[:]
        onesB = sb("onesB", [128, 512], BF16)[:]
        maskF = sb("maskF", [128, 128], F32)[:]
        negB = sb("negB", [128, 128], BF16)[:]
        PAR = sb("PAR", [128, NPAR], F32)[:]
        APAR = sb("APAR", [128, 128], F32)[:]
        CST = sb("CST", [128, 8, 32], F32)[:]
        MKT = sb("MKT", [128, 8, MEM], BF16)[:]
        MV = sb("MV", [128, 2, 1024], BF16)[:]
        CELL = sb("CELL", [128, 64], F32)[:]

        PS = [es.enter_context(nc.psum_tensor("PS%d" % i, [128, 512], F32))[:] for i in range(8)]
        PSB16 = [a.bitcast(BF16) for a in PS]

        bR = [Buf("R%d" % c) for c in range(NCH)]
        bXT = [Buf("XT%d" % c) for c in range(NCH)]
        bKT = Buf("KT"); bVV = Buf("VV"); bVV16 = Buf("VV16")
        bWS = [Buf("WS%d" % i) for i in range(NW)]
        dWS = [S.new_dsem("dws%d" % i) for i in range(NW)]
        bSTGA = Buf("STGA"); bSTGB = Buf("STGB")
        dSTGA = S.new_dsem("dstga"); dSTGB = S.new_dsem("dstgb")
        dOUTA = S.new_dsem("douta"); dOUTB = S.new_dsem("doutb")
        dCONST = S.new_dsem("dconst"); dVC = S.new_dsem("dvc"); dMC = S.new_dsem("dmc")
        bHP = [Buf("HP0"), Buf("HP1")]
        bSCR = [Buf("SCR%d" % i) for i in range(NSB)]
        bCONST = Buf("CONST"); bCST = Buf("CST"); bMKT = Buf("MKT"); bMV = Buf("MV")
        bCELL = [Buf("CELL%d" % i) for i in range(64)]
        bPS = [Buf("PS%d" % i) for i in range(8)]

        st = {"bank": 0, "w": 0, "cell": 0, "stg": 0}
        dSCR = [S.new_dsem("dscr%d" % i) for i in range(NSB)]

        def next_stg(ring):
            i = ring[st["stg"] % len(ring)]
            st["stg"] += 1
            return i

        RING_IO = list(range(0, 16))
        RING_KV = list(range(0, 8))

        def next_bank(exclude=None):
            i = st["bank"]
            if exclude is not None and i == exclude:
                i = (i + 1) % 8
            st["bank"] = (i + 1) % 8
            return i

        def next_cell():
            i = st["cell"]
            st["cell"] = (i + 1) % 64
            return i

        S.dma("sp", PAR, params, dCONST, writes=[bCONST])
        S.dma("sp", identF, cident, dCONST, writes=[bCONST])
        S.dma("sp", maskF, cmask, dCONST, writes=[bCONST])
        S.op("dve", lambda e: e.memset(onesF, 1.0), writes=[bCONST])
        S.op("dve", lambda e: e.memset(onesB, 1.0), writes=[bCONST])
        S.op("dve", lambda e: e.tensor_copy(out=identB, in_=identF), reads=[bCONST], writes=[bCONST])
        S.op("dve", lambda e: e.tensor_scalar(out=negB, in0=maskF, scalar1=-1.0, scalar2=30000.0,
                                              op0=ALU.add, op1=ALU.mult),
             reads=[bCONST], writes=[bCONST])
        S.op("dve", lambda e: e.tensor_scalar(out=APAR, in0=PAR[:, 0:128], scalar1=ALPHA, scalar2=None, op0=ALU.mult),
             reads=[bCONST], writes=[bCONST])

        def par_ln(i):
            return (PAR[:, 32 * i:32 * i + 16], PAR[:, 32 * i + 16:32 * i + 32],
                    APAR[:, 32 * i:32 * i + 16], APAR[:, 32 * i + 16:32 * i + 32])

        CWP = PAR[:, 128:128 + 8 * CW]
        CBP = PAR[:, 128 + 8 * CW:128 + 8 * CW + 8]
        CGP = PAR[:, 128 + 8 * CW + 8:128 + 8 * CW + 16]
        CLB = PAR[:, 128 + 8 * CW + 16:128 + 8 * CW + 24]

        NLMAX = 176
        wcache = nc.dram_tensor("wcache", [NLMAX, 128, 8, 512], BF16, kind="Internal").ap()
        bWC = [Buf("WC%d" % i) for i in range(NLMAX)]
        dWB = [S.new_dsem("dwb%d" % i) for i in range(NW)]
        dWSH = [S.new_dsem("dwsh%d" % i) for i in range(NW)]
        st["wmode"] = "direct"
        st["li"] = 0

        def load_w(W, k0, kn, segs):
            i = st["w"] % NW
            st["w"] += 1
            if st["wmode"] == "cached":
                li = st["li"]
                st["li"] += 1
                S.dma("sp", WS[i], wcache[li], dWSH[i], reads=[bWC[li]], writes=[bWS[i]])
                return i
            off = 0
            for (col, wd) in segs:
                src = W[k0 * 128:(k0 + kn) * 128, col:col + wd].rearrange("(k p) n -> p k n", p=128)
                S.dma("pool", WS[i][:, 0:kn, off:off + wd], src, dWS[i], writes=[bWS[i]])
                off += wd
            if st["wmode"] == "fill":
                li = st["li"]
                st["li"] += 1
                assert li < NLMAX
                S.dma("sp", wcache[li], WS[i], dWB[i], reads=[bWS[i]], writes=[bWC[li]])
            return i

        def linear_fm(W, KC, rhs_fn, rhs_bufs_fn, groups, NT, epilogue):
            nks = (KC + 7) // 8
            for gi, segs in enumerate(groups):
                width = sum(w for _, w in segs)
                nchk = width // 128
                banks = [next_bank() for _ in range(nchk)]
                for ks in range(nks):
                    kn = min(8, KC - ks * 8)
                    si = load_w(W, ks * 8, kn, segs)
                    for j in range(nchk):
                        for k in range(kn):
                            kc = ks * 8 + k
                            last = kc == KC - 1
                            S.op("pe", lambda e, j=j, k=k, kc=kc, last=last, si=si: e.matmul(
                                out=PS[banks[j]][:, 0:NT], lhsT=WS[si][:, k, j * 128:(j + 1) * 128],
                                rhs=rhs_fn(kc), start=(kc == 0), stop=last),
                                 reads=[bWS[si]] + rhs_bufs_fn(kc), writes=[bPS[banks[j]]], signal=last)
                epilogue(gi, banks)

        def linear_tm(W, KC, lhs_fn, lhs_bufs_fn, cols, NT, epilogue):
            TB = min(NT, 128)
            NBLK = (NT + 127) // 128
            nks = (KC + 7) // 8
            for gi, col in enumerate(cols):
                banks = [next_bank() for _ in range(NBLK)]
                for ks in range(nks):
                    kn = min(8, KC - ks * 8)
                    si = load_w(W, ks * 8, kn, [(col, 512)])
                    for b in range(NBLK):
                        for k in range(kn):
                            kc = ks * 8 + k
                            last = kc == KC - 1
                            S.op("pe", lambda e, b=b, k=k, kc=kc, last=last, si=si: e.matmul(
                                out=PS[banks[b]][0:TB, 0:512], lhsT=lhs_fn(kc, b),
                                rhs=WS[si][:, k, 0:512], start=(kc == 0), stop=last),
                                 reads=[bWS[si]] + lhs_bufs_fn(kc), writes=[bPS[banks[b]]], signal=last)
                epilogue(gi, banks)

        def tm_to_fm(src, src_bufs, nrows, ncols, evac):
            nchk = ncols // 128
            for g in range((nchk + 3) // 4):
                bk = next_bank()
                nj = min(4, nchk - g * 4)
                for j in range(nj):
                    c = g * 4 + j
                    S.op("pe", lambda e, c=c, j=j, bk=bk: e.matmul(
                        out=PS[bk][:, j * 128:j * 128 + nrows], lhsT=src[0:nrows, c * 128:(c + 1) * 128],
                        rhs=identF[0:nrows, 0:nrows], start=True, stop=True),
                         reads=src_bufs + [bCONST], writes=[bPS[bk]])
                evac(g, bk, nj)

        def v3(ap2d, nj, n):
            return ap2d[:, 0:nj * 128].rearrange("p (j q) -> p j q", q=128)[:, :, 0:n]

        LNB = 16

        def layer_norm(src_fn, src_bufs_fn, nchk, NT, out_fn, tb=None):
            tb = tb or [LNB, LNB + 1, LNB + 2, LNB + 3, LNB + 4]
            SQ = [SCR[tb[0]], SCR[tb[1]]]; bSQ = [bSCR[tb[0]], bSCR[tb[1]]]
            TMP = SQ; bTMP = bSQ
            MEAN = SCR[tb[2]]; bMEAN = bSCR[tb[2]]
            RSTD = SCR[tb[3]]; bRSTD = bSCR[tb[3]]
            NMR = SCR[tb[4]]; bNMR = bSCR[tb[4]]
            b1 = next_bank(); b2 = next_bank()
            dim = float(nchk * 128)
            for c in range(nchk):
                last = c == nchk - 1
                S.op("pe", lambda e, c=c, last=last: e.matmul(out=PS[b1][:, 0:NT], lhsT=onesF, rhs=src_fn(c),
                                                               start=(c == 0), stop=last),
                     reads=[bCONST] + src_bufs_fn(c), writes=[bPS[b1]])
                S.op("act", lambda e, c=c: e.activation(out=SQ[c % 2][:, 0:NT], in_=src_fn(c), func=AF.Square),
                     reads=src_bufs_fn(c), writes=[bSQ[c % 2]])
                S.op("pe", lambda e, c=c, last=last: e.matmul(out=PS[b2][:, 0:NT], lhsT=onesF,
                                                               rhs=SQ[c % 2][:, 0:NT],
                                                               start=(c == 0), stop=last),
                     reads=[bCONST, bSQ[c % 2]], writes=[bPS[b2]])
            S.op("act", lambda e: e.activation(out=MEAN[:, 0:NT], in_=PS[b1][:, 0:NT], func=AF.Copy, scale=1.0 / dim),
                 reads=[bPS[b1]], writes=[bMEAN])
            S.op("dve", lambda e: e.tensor_tensor(out=RSTD[:, 0:NT], in0=MEAN[:, 0:NT], in1=MEAN[:, 0:NT], op=ALU.mult),
                 reads=[bMEAN], writes=[bRSTD])
            S.op("dve", lambda e: e.scalar_tensor_tensor(out=RSTD[:, 0:NT], in0=PS[b2][:, 0:NT], scalar=1.0 / dim,
                                                         in1=RSTD[:, 0:NT], op0=ALU.mult, op1=ALU.subtract),
                 reads=[bPS[b2]], writes=[bRSTD])
            S.op("act", lambda e: e.activation(out=RSTD[:, 0:NT], in_=RSTD[:, 0:NT], func=AF.Ln, bias=LN_EPS, scale=1.0),
                 writes=[bRSTD])
            S.op("act", lambda e: e.activation(out=RSTD[:, 0:NT], in_=RSTD[:, 0:NT], func=AF.Exp, scale=-0.5),
                 writes=[bRSTD])
            S.op("dve", lambda e: e.scalar_tensor_tensor(out=NMR[:, 0:NT], in0=MEAN[:, 0:NT], scalar=-1.0,
                                                         in1=RSTD[:, 0:NT], op0=ALU.mult, op1=ALU.mult),
                 reads=[bMEAN, bRSTD], writes=[bNMR])
            for c in range(nchk):
                t = TMP[c % 2][:, 0:NT]
                S.op("dve", lambda e, c=c, t=t: e.tensor_tensor(out=t, in0=src_fn(c), in1=RSTD[:, 0:NT], op=ALU.mult),
                     reads=src_bufs_fn(c) + [bRSTD], writes=[bTMP[c % 2]])
                S.op("dve", lambda e, t=t: e.tensor_tensor(out=t, in0=t, in1=NMR[:, 0:NT], op=ALU.add),
                     reads=[bNMR], writes=[bTMP[c % 2]])
                out_fn(c, t, bTMP[c % 2])

        def ln_to_stream(i, NT):
            g, b, ag, ab = par_ln(i)

            def out_fn(c, t, bt):
                S.op("act", lambda e: e.activation(out=R[:, c, 0:NT], in_=t, func=AF.Identity,
                                                   scale=ag[:, c:c + 1], bias=ab[:, c:c + 1]),
                     reads=[bt, bCONST], writes=[bR[c]])
                S.op("act", lambda e: e.activation(out=XT[:, c, 0:NT], in_=t, func=AF.Identity,
                                                   scale=g[:, c:c + 1], bias=b[:, c:c + 1]),
                     reads=[bt, bCONST], writes=[bXT[c]])

            layer_norm(lambda c: R[:, c, 0:NT], lambda c: [bR[c]], NCH, NT, out_fn)

        def xt_rhs(NT):
            return (lambda kc: XT[:, kc, 0:NT]), (lambda kc: [bXT[kc]])

        def ffn(Wg, Wu, Wd, NT):
            rhs_fn, rhs_bufs = xt_rhs(NT)
            fgs = []
            f = 0
            while f < FC:
                n = min(8, FC - f)
                fgs.append((f, n))
                f += n
            for fi, (f0, fn_) in enumerate(fgs):
                hb = (fi % 2) * 4
                gb = 8 + (fi % 2) * 4

                def HTc(j):
                    return SCRB[hb + j // 2][:, (j % 2) * 512:(j % 2) * 512 + NT]

                def bHTc(j):
                    return bSCR[hb + j // 2]

                sub = []
                j0 = 0
                while j0 < fn_:
                    sub.append((j0, min(4, fn_ - j0)))
                    j0 += 4
                for (j0, jn) in sub:
                    col = (f0 + j0) * 128

                    def ep_gate(gi, banks, j0=j0, jn=jn):
                        for j in range(jn):
                            S.op("act", lambda e, j=j: e.activation(out=SCR[gb + j][:, 0:NT], in_=PS[banks[j]][:, 0:NT],
                                                                     func=AF.Silu),
                                 reads=[bPS[banks[j]]], writes=[bSCR[gb + j]])

                    def ep_up(gi, banks, j0=j0, jn=jn):
                        for j in range(jn):
                            S.op("dve", lambda e, j=j: e.tensor_tensor(out=HTc(j0 + j), in0=PS[banks[j]][:, 0:NT],
                                                                        in1=SCR[gb + j][:, 0:NT], op=ALU.mult),
                                 reads=[bPS[banks[j]], bSCR[gb + j]], writes=[bHTc(j0 + j)])

                    linear_fm(Wg, NCH, rhs_fn, rhs_bufs, [[(col, jn * 128)]], NT, ep_gate)
                    linear_fm(Wu, NCH, rhs_fn, rhs_bufs, [[(col, jn * 128)]], NT, ep_up)

                def ep_down(gi, banks):
                    for j in range(4):
                        c = gi * 4 + j
                        S.op("dve", lambda e, j=j, c=c: e.scalar_tensor_tensor(
                            out=R[:, c, 0:NT], in0=PS[banks[j]][:, 0:NT], scalar=0.5, in1=R[:, c, 0:NT],
                            op0=ALU.mult, op1=ALU.add),
                             reads=[bPS[banks[j]]], writes=[bR[c]])

                Wd_g = Wd[f0 * 128:(f0 + fn_) * 128, :]
                linear_fm(Wd_g, fn_, lambda kc: HTc(kc), lambda kc: [bHTc(kc)],
                          [[(g * 512, 512)] for g in range(4)], NT, ep_down)

        def load_x(x_ap, t0, NT):
            TB = min(NT, 128)
            for b in range((NT + 127) // 128):
                for g in range(4):
                    si = next_stg(RING_IO)
                    S.dma("sp", SCR[si][0:TB, :], x_ap[t0 + b * 128:t0 + b * 128 + TB, g * 512:(g + 1) * 512], dSCR[si],
                          writes=[bSCR[si]])
                    bk = next_bank()
                    for j in range(4):
                        S.op("pe", lambda e, j=j: e.matmul(
                            out=PS[bk][:, j * 128:j * 128 + TB], lhsT=SCR[si][0:TB, j * 128:(j + 1) * 128],
                            rhs=identF[0:TB, 0:TB], start=True, stop=True),
                             reads=[bSCR[si], bCONST], writes=[bPS[bk]])
                    c0 = g * 4
                    S.op("act", lambda e: e.activation(out=R[:, c0:c0 + 4, b * 128:b * 128 + TB],
                                                       in_=v3(PS[bk], 4, TB), func=AF.Copy, scale=ALPHA),
                         reads=[bPS[bk]], writes=[bR[c0 + j] for j in range(4)])
                    S.op("dve", lambda e: e.tensor_copy(out=XT[:, c0:c0 + 4, b * 128:b * 128 + TB],
                                                        in_=v3(PS[bk], 4, TB)),
                         reads=[bPS[bk]], writes=[bXT[c0 + j] for j in range(4)])

        def store_fm(src_fn, src_bufs_fn, nchk, NT, out_ap, t0):
            TB = min(NT, 128)
            k = 0
            for b in range((NT + 127) // 128):
                for g in range(nchk // 4):
                    bk = next_bank()
                    for j in range(4):
                        c = g * 4 + j
                        S.op("pe", lambda e, c=c, j=j: e.matmul(
                            out=PS[bk][0:TB, j * 128:(j + 1) * 128], lhsT=src_fn(c)[:, b * 128:b * 128 + TB],
                            rhs=identF, start=True, stop=True),
                             reads=src_bufs_fn(c) + [bCONST], writes=[bPS[bk]])
                    si = next_stg(RING_IO)
                    if k % 2 == 0:
                        S.op("act", lambda e: e.activation(out=SCR[si][0:TB, :], in_=PS[bk][0:TB, :], func=AF.Copy),
                             reads=[bPS[bk]], writes=[bSCR[si]])
                    else:
                        S.op("dve", lambda e: e.tensor_copy(out=SCR[si][0:TB, :], in_=PS[bk][0:TB, :]),
                             reads=[bPS[bk]], writes=[bSCR[si]])
                    k += 1
                    S.dma("sp", out_ap[t0 + b * 128:t0 + b * 128 + TB, g * 512:(g + 1) * 512], SCR[si][0:TB, :], dSCR[si],
                          reads=[bSCR[si]], is_out=True)

        def store_conv_state(out_ap):
            for g in range(2):
                bk = next_bank()
                for j in range(4):
                    c = g * 4 + j
                    S.op("pe", lambda e, c=c, j=j, bk=bk: e.matmul(
                        out=PS[bk][0:CS, j * 128:(j + 1) * 128], lhsT=CST[:, c, 0:CS], rhs=identF, start=True, stop=True),
                         reads=[bCST, bCONST], writes=[bPS[bk]])
                S.op("act", lambda e, g=g, bk=bk: e.activation(out=STGB[0:CS, g * 512:(g + 1) * 512],
                                                               in_=PS[bk][0:CS, 0:512], func=AF.Copy),
                     reads=[bPS[bk]], writes=[bSTGB])
            S.dma("sp", out_ap[:, :], STGB[0:CS, :], dOUTB, reads=[bSTGB], is_out=True)

        def load_conv_cache():
            S.dma("sp", STGB[0:CS, :], c_conv[:, :], dSTGB, writes=[bSTGB])
            bk = next_bank()
            for c in range(8):
                S.op("pe", lambda e, c=c: e.matmul(out=PS[bk][:, c * 32:c * 32 + CS],
                                                    lhsT=STGB[0:CS, c * 128:(c + 1) * 128],
                                                    rhs=identF[0:CS, 0:CS], start=True, stop=True),
                     reads=[bSTGB, bCONST], writes=[bPS[bk]])
            S.op("act", lambda e: e.activation(
                out=CST[:, :, 0:CS], in_=PS[bk][:, 0:256].rearrange("p (c t) -> p c t", t=32)[:, :, 0:CS], func=AF.Copy),
                 reads=[bPS[bk]], writes=[bCST])

        def k_rows_to_KT(src, src_bufs, nrows, pos):
            def evac(g, bk, nj):
                S.op("dve", lambda e: e.tensor_copy(out=KT[:, g * 4:g * 4 + nj, pos:pos + nrows], in_=v3(PS[bk], nj, nrows)),
                     reads=[bPS[bk]], writes=[bKT])
            tm_to_fm(src, src_bufs, nrows, 1024, evac)

        def load_kv_cache():
            S.dma("pool", VV[:, 0:PAST // 128, :], c_v.rearrange("(b p) n -> p b n", p=128), dVC, writes=[bVV])
            for blk in range(PAST // 128):
                S.dma("sp", STGB[:, :], c_k[blk * 128:(blk + 1) * 128, :], dSTGB, writes=[bSTGB])
                k_rows_to_KT(STGB, [bSTGB], 128, blk * 128)

        def mem_rows_to_MKT(src, src_bufs, blk):
            def evac(g, bk, nj):
                S.op("dve", lambda e: e.tensor_copy(out=MKT[:, g * 4:g * 4 + nj, blk * 128:(blk + 1) * 128],
                                                    in_=v3(PS[bk], nj, 128)),
                     reads=[bPS[bk]], writes=[bMKT])
            tm_to_fm(src, src_bufs, 128, 1024, evac)

        def load_mem_cache():
            S.dma("pool", MV[:, :, :], c_mv.rearrange("(b p) n -> p b n", p=128), dMC, writes=[bMV])
            for blk in range(2):
                S.dma("sp", STGB[:, :], c_mk[blk * 128:(blk + 1) * 128, :], dSTGB, writes=[bSTGB])
                mem_rows_to_MKT(STGB, [bSTGB], blk)

        def prompt_mem_kv():
            def memT(c):
                return SCRB[c // 4][:, (c % 4) * 256:(c % 4) * 256 + MEM]

            for b in range(2):
              for hh in range(2):
                S.dma("sp", STGA[:, :], mem_p[b * 128:(b + 1) * 128, hh * 1024:(hh + 1) * 1024], dSTGA, writes=[bSTGA])

                def evac(g, bk, nj, b=b, hh=hh):
                    gq = hh * 2 + g
                    dst = SCRB[gq][:, 0:1024].rearrange("p (j q) -> p j q", q=256)[:, 0:nj, b * 128:(b + 1) * 128]
                    S.op("dve", lambda e: e.tensor_copy(out=dst, in_=v3(PS[bk], nj, 128)),
                         reads=[bPS[bk]], writes=[bSCR[gq]])
                tm_to_fm(STGA, [bSTGA], 128, 1024, evac)

            for (W, out_ap, is_k) in ((wk, mk_p, True), (wv, mv_p, False)):
                def ep(gi, banks, out_ap=out_ap, is_k=is_k):
                    for b in range(2):
                        S.op("act", lambda e, b=b: e.activation(out=STGB[:, 0:512], in_=PS[banks[b]][:, 0:512], func=AF.Copy),
                             reads=[bPS[banks[b]]], writes=[bSTGB])
                        S.dma("sp", out_ap[b * 128:(b + 1) * 128, gi * 512:(gi + 1) * 512], STGB[:, 0:512], dOUTB,
                              reads=[bSTGB], is_out=True)
                        if is_k:
                            def evac(g, bk, nj, b=b):
                                S.op("dve", lambda e: e.tensor_copy(
                                    out=MKT[:, gi * 4:gi * 4 + nj, b * 128:(b + 1) * 128], in_=v3(PS[bk], nj, 128)),
                                     reads=[bPS[bk]], writes=[bMKT])
                            tm_to_fm(STGB, [bSTGB], 128, 512, evac)
                        else:
                            S.op("dve", lambda e, b=b: e.tensor_copy(out=MV[:, b, gi * 512:(gi + 1) * 512],
                                                                     in_=PS[banks[b]][:, 0:512]),
                                 reads=[bPS[banks[b]]], writes=[bMV])
                linear_tm(W, NCH, lambda kc, b: memT(kc)[:, b * 128:(b + 1) * 128], lambda kc: [bSCR[kc // 4]],
                          [0, 512], MEM, ep)

        QTB = 16

        def QTh(h, NT):
            return SCRB[QTB + h // 2][:, (h % 2) * 512:(h % 2) * 512 + NT]

        def proj_qkv(NT, pos0, k_out, v_out, t0):
            TB = min(NT, 128)
            NBLK = (NT + 127) // 128
            rhs_fn, rhs_bufs = xt_rhs(NT)

            def ep_q(gi, banks):
                for j in range(4):
                    h = gi * 4 + j
                    S.op("act", lambda e, j=j, h=h: e.activation(out=QTh(h, NT), in_=PS[banks[j]][:, 0:NT], func=AF.Copy),
                         reads=[bPS[banks[j]]], writes=[bSCR[QTB + h // 2]])
            linear_fm(w_in, NCH, rhs_fn, rhs_bufs, [[(2048, 512)], [(2560, 512)]], NT, ep_q)

            lhs_fn = lambda kc, b: XT[:, kc, b * 128:b * 128 + TB]
            lhs_bufs = lambda kc: [bXT[kc]]

            def ep_k(gi, banks):
                for b in range(NBLK):
                    si = next_stg(RING_KV)
                    S.op("act", lambda e, b=b: e.activation(out=SCR[si][0:TB, 0:512], in_=PS[banks[b]][0:TB, 0:512], func=AF.Copy),
                         reads=[bPS[banks[b]]], writes=[bSCR[si]])
                    S.dma("sp", k_out[t0 + b * 128:t0 + b * 128 + TB, gi * 512:(gi + 1) * 512], SCR[si][0:TB, 0:512], dSCR[si],
                          reads=[bSCR[si]], is_out=True)

                    def evac(g, bk, nj, b=b):
                        S.op("act", lambda e: e.activation(
                            out=KT[:, gi * 4:gi * 4 + nj, pos0 + b * 128:pos0 + b * 128 + TB], in_=v3(PS[bk], nj, TB),
                            func=AF.Copy),
                             reads=[bPS[bk]], writes=[bKT])
                    tm_to_fm(SCR[si], [bSCR[si]], TB, 512, evac)
            linear_tm(w_in, NCH, lhs_fn, lhs_bufs, [3072, 3584], NT, ep_k)

            def ep_v(gi, banks):
                for b in range(NBLK):
                    si = next_stg(RING_KV)
                    S.op("act", lambda e, b=b: e.activation(out=SCR[si][0:TB, 0:512], in_=PS[banks[b]][0:TB, 0:512], func=AF.Copy),
                         reads=[bPS[banks[b]]], writes=[bSCR[si]])
                    S.dma("sp", v_out[t0 + b * 128:t0 + b * 128 + TB, gi * 512:(gi + 1) * 512], SCR[si][0:TB, 0:512], dSCR[si],
                          reads=[bSCR[si]], is_out=True)
                    S.op("act", lambda e, b=b: e.activation(
                        out=VV[0:TB, (pos0 + b * 128) // 128, gi * 512:(gi + 1) * 512], in_=PS[banks[b]][0:TB, 0:512],
                        func=AF.Copy),
                         reads=[bPS[banks[b]]], writes=[bVV, bVV16])
            linear_tm(w_in, NCH, lhs_fn, lhs_bufs, [4096, 4608], NT, ep_v)

        def MIXc(c, NT):
            return SCRB[c // 2][:, (c % 2) * 512:(c % 2) * 512 + NT]

        def emit_pipelined(chains):
            if not chains:
                return
            nst = max(len(c) for c in chains)
            for step in range(len(chains) + nst - 1):
                for sg in range(nst - 1, -1, -1):
                    i = step - sg
                    if 0 <= i < len(chains) and sg < len(chains[i]):
                        chains[i][sg]()

        pools = {"s": [0, 1, 2], "t": [3, 4], "o": [5, 6, 7]}
        pool_i = {"s": 0, "t": 0, "o": 0}

        def pool_bank(p):
            b = pools[p][pool_i[p] % len(pools[p])]
            pool_i[p] += 1
            return b

        def sb_attention(NT, pos0):
            TB = min(NT, 128)
            scale = 128 ** -0.5
            sets = []
            for base in (8, 12):
                sets.append(dict(E=SCR[base], SP=SCR[base + 1], PN=SCR[base + 2], AB=SCRB[base + 3],
                                 bufs=[bSCR[base + i] for i in range(4)]))
            sets.append(dict(E=STGA[:, 0:512], SP=STGA[:, 512:1024], PN=HP[:, 0, 0:512], AB=SCRB[20],
                             bufs=[bSTGA, bHP[0], bSCR[20]]))
            if NT == TT and NVB * 128 > SEQ:
                sets.append(dict(E=STGB[:, 0:512], SP=STGB[:, 512:1024], PN=HP[:, 1, 0:512], AB=VV[:, NVB - 1, :],
                                 bufs=[bSTGB, bHP[1], bVV16]))
            NSETS = len(sets)
            for st_ in sets:
                S.op("dve", lambda e, st_=st_: e.memset(st_["PN"][:, 0:1], 0.0), writes=st_["bufs"])
            chains = []
            ci = 0
            for qb in range((NT + 127) // 128):
                g0 = pos0 + qb * 128
                if TB == 128:
                    s1 = max(0, g0 + TB - 512)
                    chunks = [(s1, g0 + TB - s1, True)]
                    end = s1
                else:
                    chunks = [(g0, TB, True)]
                    end = g0
                while end > 0:
                    s0 = max(0, end - 512)
                    chunks.append((s0, end - s0, False))
                    end = s0
                nblk_total = sum((n + 127) // 128 for (_, n, _) in chunks)
                for h in range(SBH):
                    hs = dict(ob=None, prev_nb=None, blk_i=0)
                    for ck, (s, n, diag) in enumerate(chunks):
                        cs = dict(set=sets[ci % NSETS])
                        ci += 1
                        last_chunk = ck == len(chunks) - 1

                        def mk(qb=qb, h=h, s=s, n=n, diag=diag, hs=hs, cs=cs, last_chunk=last_chunk,
                               nblk_total=nblk_total):
                            st_ = cs["set"]
                            bset = st_["bufs"]
                            E = st_["E"][0:TB, 0:n]
                            SP = st_["SP"][0:TB, 0:n]
                            PN = st_["PN"]
                            A = st_["AB"][0:TB, 0:n]
                            AT = st_["AB"][:, 512:1024]
                            qT = QTh(h, NT)[:, qb * 128:qb * 128 + TB]
                            bq = bSCR[QTB + h // 2]
                            nkb = (n + 127) // 128

                            def st0():
                                cs["sbk"] = sbk = pool_bank("s")
                                S.op("pe", lambda e: e.matmul(out=PS[sbk][0:TB, 0:n], lhsT=qT, rhs=KT[:, h, s:s + n],
                                                              start=True, stop=(not diag)),
                                     reads=[bq, bKT], writes=[bPS[sbk]], signal=(not diag))
                                if diag:
                                    S.op("pe", lambda e: e.matmul(out=PS[sbk][0:TB, n - TB:n], lhsT=identB[0:TB, 0:TB],
                                                                  rhs=negB[0:TB, 0:TB], start=False, stop=True),
                                         reads=[bCONST], writes=[bPS[sbk]])

                            def st1():
                                sbk = cs["sbk"]
                                S.op("act", lambda e: e.activation(out=E, in_=PS[sbk][0:TB, 0:n], func=AF.Exp, scale=scale),
                                     reads=[bPS[sbk]], writes=bset)
                                cs["ct"] = ct = next_cell()
                                S.op("act", lambda e: e.activation(out=SP, in_=E, func=AF.Ln, bias=1.0, scale=1.0,
                                                                   accum_out=CELL[0:TB, ct:ct + 1]),
                                     writes=bset + [bCELL[ct]])

                            def st2():
                                ct = cs["ct"]
                                if n > 1:
                                    S.op("dve", lambda e: e.tensor_tensor_scan(
                                        out=PN[0:TB, 1:n], data0=onesB[0:TB, 0:n - 1], data1=st_["SP"][0:TB, 0:n - 1],
                                        initial=0.0, op0=ALU.mult, op1=ALU.add),
                                         reads=[bCONST], writes=bset)
                                cs["cn"] = cn = next_cell()
                                if hs["prev_nb"] is None:
                                    S.op("dve", lambda e: e.tensor_scalar(out=CELL[0:TB, cn:cn + 1], in0=CELL[0:TB, ct:ct + 1],
                                                                          scalar1=-1.0, scalar2=None, op0=ALU.mult),
                                         reads=[bCELL[ct]], writes=[bCELL[cn]])
                                else:
                                    pn_ = hs["prev_nb"]
                                    S.op("dve", lambda e: e.tensor_tensor(out=CELL[0:TB, cn:cn + 1],
                                                                          in0=CELL[0:TB, pn_:pn_ + 1],
                                                                          in1=CELL[0:TB, ct:ct + 1], op=ALU.subtract),
                                         reads=[bCELL[ct], bCELL[pn_]], writes=[bCELL[cn]])
                                hs["prev_nb"] = cn

                            def st3():
                                cn = cs["cn"]
                                S.op("act", lambda e: e.activation(out=SP, in_=PN[0:TB, 0:n], func=AF.Exp, scale=1.0,
                                                                   bias=CELL[0:TB, cn:cn + 1]),
                                     reads=[bCELL[cn]], writes=bset)
                                S.op("pool", lambda e: e.tensor_tensor(out=A, in0=E, in1=SP, op=ALU.mult), writes=bset)

                            def st4():
                                cs["tbk"] = tbk = pool_bank("t")
                                for kb in range(nkb):
                                    kn = min(128, n - kb * 128)
                                    S.op("pe", lambda e, kb=kb, kn=kn: e.transpose(
                                        out=PSB16[tbk][0:kn, kb * 128:kb * 128 + TB],
                                        in_=st_["AB"][0:TB, kb * 128:kb * 128 + kn], identity=identB[0:TB, 0:TB]),
                                         reads=bset + [bCONST], writes=[bPS[tbk]])
                                kn0 = min(128, n)
                                S.op("dve", lambda e: e.tensor_copy(
                                    out=AT[0:kn0, 0:nkb * 128].rearrange("p (k q) -> p k q", q=128)[:, :, 0:TB],
                                    in_=PSB16[tbk][0:kn0, 0:nkb * 128].rearrange("p (k q) -> p k q", q=128)[:, :, 0:TB]),
                                     reads=[bPS[tbk]], writes=bset)

                            def st5():
                                if hs["ob"] is None:
                                    hs["ob"] = pool_bank("o")
                                ob = hs["ob"]
                                for kb in range(nkb):
                                    kn = min(128, n - kb * 128)
                                    first = hs["blk_i"] == 0
                                    last = hs["blk_i"] == nblk_total - 1
                                    S.op("pe", lambda e, kb=kb, kn=kn, first=first, last=last: e.matmul(
                                        out=PS[ob][:, 0:TB], lhsT=VV[0:kn, (s + kb * 128) // 128, h * 128:(h + 1) * 128],
                                        rhs=AT[0:kn, kb * 128:kb * 128 + TB], start=first, stop=last),
                                         reads=bset + [bVV], writes=[bPS[ob]])
                                    hs["blk_i"] += 1
                                if last_chunk:
                                    S.op("act", lambda e: e.activation(out=MIXc(8 + h, NT)[:, qb * 128:qb * 128 + TB],
                                                                       in_=PS[ob][:, 0:TB], func=AF.Copy),
                                         reads=[bPS[ob]], writes=[bSCR[(8 + h) // 2]])

                            return [st0, st1, st2, st3, st4, st5]

                        chains.append(mk())
            emit_pipelined(chains)

        def conv_module(NT, first_tile):
            rhs_fn, rhs_bufs = xt_rhs(NT)
            YB = 8
            SIGB = 16

            def ep_pair(gi, banks):
                for i in range(2):
                    c = gi * 2 + i
                    hp = HP[:, i, :]
                    if first_tile:
                        S.op("dve", lambda e, hp=hp: e.memset(hp[:, 0:CS], 0.0), writes=[bHP[i]])
                    else:
                        S.op("act", lambda e, hp=hp, c=c: e.activation(out=hp[:, 0:CS], in_=CST[:, c, 0:CS], func=AF.Copy),
                             reads=[bCST], writes=[bHP[i]])
                    S.op("act", lambda e, i=i: e.activation(out=SCR[SIGB + i][:, 0:NT], in_=PS[banks[2 + i]][:, 0:NT],
                                                             func=AF.Sigmoid),
                         reads=[bPS[banks[2 + i]]], writes=[bSCR[SIGB + i]])
                    S.op("dve", lambda e, i=i, hp=hp: e.tensor_tensor(out=hp[:, CS:CS + NT], in0=PS[banks[i]][:, 0:NT],
                                                                       in1=SCR[SIGB + i][:, 0:NT], op=ALU.mult),
                         reads=[bPS[banks[i]], bSCR[SIGB + i]], writes=[bHP[i]])
                    S.op("act", lambda e, hp=hp, c=c: e.activation(out=CST[:, c, 0:CS], in_=hp[:, NT:NT + CS], func=AF.Copy),
                         reads=[bHP[i]], writes=[bCST])
                for j in range(CW):
                    for i in range(2):
                        c = gi * 2 + i
                        hp = HP[:, i, :]
                        y = SCR[YB + c][:, 0:NT]
                        wcol = CWP[:, c * CW + j:c * CW + j + 1]
                        teng = "dve"
                        if j == 0:
                            S.op(teng, lambda e, hp=hp, y=y, wcol=wcol, c=c: e.tensor_scalar(
                                out=y, in0=hp[:, 0:NT], scalar1=wcol, scalar2=CBP[:, c:c + 1], op0=ALU.mult, op1=ALU.add),
                                 reads=[bHP[i], bCONST], writes=[bSCR[YB + c]])
                        else:
                            S.op(teng, lambda e, hp=hp, y=y, wcol=wcol, j=j: e.scalar_tensor_tensor(
                                out=y, in0=hp[:, j:j + NT], scalar=wcol, in1=y, op0=ALU.mult, op1=ALU.add),
                                 reads=[bHP[i], bCONST], writes=[bSCR[YB + c]])

            groups = [[(c0 * 128, 256), (1024 + c0 * 128, 256)] for c0 in range(0, 8, 2)]
            linear_fm(w_in, NCH, rhs_fn, rhs_bufs, groups, NT, ep_pair)

        def conv_norm(NT):
            YB = 8

            def out_fn(c, t, bt):
                S.op("act", lambda e: e.activation(out=MIXc(c, NT), in_=t, func=AF.Silu,
                                                   scale=CGP[:, c:c + 1], bias=CLB[:, c:c + 1]),
                     reads=[bt, bCONST], writes=[bSCR[c // 2]])
            layer_norm(lambda c: SCR[YB + c][:, 0:NT], lambda c: [bSCR[YB + c]], 8, NT, out_fn, tb=[4, 5, 6, 7, 20])

        def cross_attention(NT):
            TB = min(NT, 128)
            rhs_fn, rhs_bufs = xt_rhs(NT)
            scale = 256 ** -0.5

            def QX(c):
                return SCRB[c // 2][:, (c % 2) * 512:(c % 2) * 512 + NT]

            def OX(c):
                return SCRB[4 + c // 2][:, (c % 2) * 512:(c % 2) * 512 + NT]

            def ep_q(gi, banks):
                for j in range(4):
                    c = gi * 4 + j
                    S.op("act", lambda e, j=j, c=c: e.activation(out=QX(c), in_=PS[banks[j]][:, 0:NT], func=AF.Copy),
                         reads=[bPS[banks[j]]], writes=[bSCR[c // 2]])
            linear_fm(wq, NCH, rhs_fn, rhs_bufs, [[(0, 512)], [(512, 512)]], NT, ep_q)

            chains = []
            it = 0
            for qb in range((NT + 127) // 128):
                for h in range(XH):
                    blk = 8 + it % 6
                    it += 1

                    def mk(qb=qb, h=h, blk=blk):
                        cs = {}
                        bs = [bSCR[blk]]
                        E = SCR[blk][0:TB, 0:MEM]
                        P = SCRB[blk][0:TB, 512:512 + MEM]
                        PT = SCRB[blk][:, 768:1024]

                        def st0():
                            cs["sbk"] = sbk = pool_bank("s")
                            for i in range(2):
                                S.op("pe", lambda e, i=i: e.matmul(
                                    out=PS[sbk][0:TB, 0:MEM], lhsT=QX(2 * h + i)[:, qb * 128:qb * 128 + TB],
                                    rhs=MKT[:, 2 * h + i, :], start=(i == 0), stop=(i == 1)),
                                     reads=[bSCR[(2 * h + i) // 2], bMKT], writes=[bPS[sbk]], signal=(i == 1))

                        def st1():
                            sbk = cs["sbk"]
                            cs["cm"] = cm = next_cell()
                            cs["cs"] = cs_ = next_cell()
                            S.op("dve", lambda e: e.reduce_max(out=CELL[0:TB, cm:cm + 1], in_=PS[sbk][0:TB, 0:MEM], axis=AX.X),
                                 reads=[bPS[sbk]], writes=[bCELL[cm]])
                            S.op("dve", lambda e: e.tensor_scalar(out=CELL[0:TB, cm:cm + 1], in0=CELL[0:TB, cm:cm + 1],
                                                                  scalar1=-scale, scalar2=None, op0=ALU.mult),
                                 writes=[bCELL[cm]])
                            S.op("act", lambda e: e.activation(out=E, in_=PS[sbk][0:TB, 0:MEM], func=AF.Exp, scale=scale,
                                                               bias=CELL[0:TB, cm:cm + 1],
                                                               accum_out=CELL[0:TB, cs_:cs_ + 1]),
                                 reads=[bPS[sbk], bCELL[cm]], writes=bs + [bCELL[cs_]])

                        def st2():
                            cs_ = cs["cs"]
                            S.op("dve", lambda e: e.reciprocal(out=CELL[0:TB, cs_:cs_ + 1], in_=CELL[0:TB, cs_:cs_ + 1]),
                                 writes=[bCELL[cs_]])
                            S.op("dve", lambda e: e.tensor_scalar(out=P, in0=E, scalar1=CELL[0:TB, cs_:cs_ + 1],
                                                                  scalar2=None, op0=ALU.mult),
                                 reads=[bCELL[cs_]], writes=bs)

                        def st3():
                            cs["tbk"] = tbk = pool_bank("t")
                            for m in range(2):
                                S.op("pe", lambda e, m=m: e.transpose(
                                    out=PSB16[tbk][:, m * 128:m * 128 + TB],
                                    in_=SCRB[blk][0:TB, 512 + m * 128:512 + (m + 1) * 128], identity=identB[0:TB, 0:TB]),
                                     reads=bs + [bCONST], writes=[bPS[tbk]])
                            S.op("act", lambda e: e.activation(
                                out=PT.rearrange("p (k q) -> p k q", q=128)[:, :, 0:TB],
                                in_=PSB16[tbk][:, 0:256].rearrange("p (k q) -> p k q", q=128)[:, :, 0:TB], func=AF.Copy),
                                 reads=[bPS[tbk]], writes=bs)

                        def st4():
                            ob = pool_bank("o")
                            for i in range(2):
                                for m in range(2):
                                    S.op("pe", lambda e, i=i, m=m: e.matmul(
                                        out=PS[ob][:, i * 128:i * 128 + TB],
                                        lhsT=MV[:, m, h * 256 + i * 128:h * 256 + (i + 1) * 128],
                                        rhs=PT[:, m * 128:m * 128 + TB], start=(m == 0), stop=(m == 1)),
                                         reads=bs + [bMV], writes=[bPS[ob]], signal=(m == 1))
                            for i in range(2):
                                S.op("dve", lambda e, i=i: e.tensor_copy(out=OX(2 * h + i)[:, qb * 128:qb * 128 + TB],
                                                                         in_=PS[ob][:, i * 128:i * 128 + TB]),
                                     reads=[bPS[ob]], writes=[bSCR[4 + (2 * h + i) // 2]])

                        return [st0, st1, st2, st3, st4]

                    chains.append(mk())
            emit_pipelined(chains)

            def ep_o(gi, banks):
                for j in range(4):
                    c = gi * 4 + j
                    S.op("dve", lambda e, j=j, c=c: e.tensor_tensor(out=R[:, c, 0:NT], in0=PS[banks[j]][:, 0:NT],
                                                                     in1=R[:, c, 0:NT], op=ALU.add),
                         reads=[bPS[banks[j]]], writes=[bR[c]])
            linear_fm(wo, 8, lambda kc: OX(kc), lambda kc: [bSCR[4 + kc // 2]],
                      [[(g * 512, 512)] for g in range(4)], NT, ep_o)

        def run_tile(x_ap, t0, NT, pos0, k_out, v_out, y_out, first_tile, conv_out):
            rhs_fn, rhs_bufs = xt_rhs(NT)
            st["li"] = 0
            st["wmode"] = "fill" if (first_tile and NT == TT) else "cached"
            load_x(x_ap, t0, NT)
            ffn(w1g, w1u, w1d, NT)
            ln_to_stream(0, NT)
            conv_module(NT, first_tile)
            proj_qkv(NT, pos0, k_out, v_out, t0)
            if conv_out is not None:
                store_conv_state(conv_out)
            conv_norm(NT)
            sb_attention(NT, pos0)

            def ep_out(gi, banks):
                for j in range(4):
                    c = gi * 4 + j
                    S.op("dve", lambda e, j=j, c=c: e.tensor_tensor(out=R[:, c, 0:NT], in0=PS[banks[j]][:, 0:NT],
                                                                     in1=R[:, c, 0:NT], op=ALU.add),
                         reads=[bPS[banks[j]]], writes=[bR[c]])
            linear_fm(w_out, NCH, lambda kc: MIXc(kc, NT), lambda kc: [bSCR[kc // 2]],
                      [[(g * 512, 512)] for g in range(4)], NT, ep_out)
            ln_to_stream(1, NT)
            cross_attention(NT)
            ln_to_stream(2, NT)
            ffn(w2g, w2u, w2d, NT)
            g4, b4, _, _ = par_ln(3)

            def out_fn(c, t, bt):
                S.op("act", lambda e: e.activation(out=R[:, c, 0:NT], in_=t, func=AF.Identity,
                                                   scale=g4[:, c:c + 1], bias=b4[:, c:c + 1]),
                     reads=[bt, bCONST], writes=[bR[c]])
            layer_norm(lambda c: R[:, c, 0:NT], lambda c: [bR[c]], NCH, NT, out_fn)
            store_fm(lambda c: R[:, c, :], lambda c: [bR[c]], NCH, NT, y_out, t0)

        prompt_mem_kv()
        for ti in range(NTILES):
            run_tile(x_p, ti * TT, TT, ti * TT, k_p, v_p, y_p, ti == 0, conv_p if ti == NTILES - 1 else None)
        load_kv_cache()
        load_mem_cache()
        load_conv_cache()
        run_tile(x_s, 0, DEC, PAST, k_s, v_s, y_s, False, conv_s)
        S.finish()
        info = dict(cnt=dict(S.cnt), nw=st["w"])
    return nc, info


_PROG = {}


def _pack_params(inp):
    def fm(v):
        v = np.asarray(v, np.float32).reshape(-1, 128)
        return np.ascontiguousarray(v.T)
    cols = []
    for g_, b_ in ((inp["ln1_g"], inp["ln1_b"]), (inp["ln2_g"], inp["ln2_b"]),
                   (inp["ln3_g"], inp["ln3_b"]), (inp["ln4_g"], inp["ln4_b"])):
        cols.append(fm(g_[0]))
        cols.append(fm(b_[0]))
    cw = np.asarray(inp["conv_w"][0], np.float32)
    cwp = cw.reshape(CW, 8, 128).transpose(2, 1, 0).reshape(128, 8 * CW)
    cols.append(np.ascontiguousarray(cwp))
    cols.append(fm(inp["conv_b"][0]))
    cols.append(fm(inp["conv_ln_g"][0]))
    cols.append(fm(inp["conv_ln_b"][0]))
    return np.ascontiguousarray(np.concatenate(cols, axis=1).astype(np.float32))


def run(inp, SEQ, PAST, DFF, TT, trace=False):
    key = (SEQ, PAST, DFF, TT)
    if key not in _PROG:
        _PROG[key] = build_program(*key)
    nc, info = _PROG[key]
    n = 8
    f = lambda a: np.ascontiguousarray(np.asarray(a, np.float32))
    shared = {
        "w1g": f(inp["ffn1_w_gate"][0]), "w1u": f(inp["ffn1_w_up"][0]), "w1d": f(inp["ffn1_w_down"][0]),
        "w_in": f(inp["w_in"][0]), "w_out": f(inp["w_out"][0]),
        "wq": f(inp["xattn_wq"][0]), "wk": f(inp["xattn_wk"][0]), "wv": f(inp["xattn_wv"][0]),
        "wo": f(inp["xattn_wo"][0]),
        "w2g": f(inp["ffn2_w_gate"][0]), "w2u": f(inp["ffn2_w_up"][0]), "w2d": f(inp["ffn2_w_down"][0]),
        "params": _pack_params(inp),
        "cident": np.eye(128, dtype=np.float32),
        "cmask": np.tril(np.ones((128, 128), np.float32), -1),
    }
    in_maps = []
    for i in range(n):
        m = dict(shared)
        m["x_p"] = f(inp["x_prompt"][i]); m["x_s"] = f(inp["x_sample"][i]); m["mem_p"] = f(inp["mem_prompt"][i])
        m["c_conv"] = f(inp["cache_conv"][0, i])
        m["c_k"] = f(inp["cache_sb_k"][0, i]).reshape(PAST, 1024)
        m["c_v"] = f(inp["cache_sb_v"][0, i]).reshape(PAST, 1024)
        m["c_mk"] = f(inp["cache_mem_k"][0, i]).reshape(MEM, 1024)
        m["c_mv"] = f(inp["cache_mem_v"][0, i]).reshape(MEM, 1024)
        in_maps.append(m)
    res = run_bass_kernel_spmd(nc, in_maps, core_ids=list(range(n)), trace=trace)
    r = res.results

    def stack(name, shape):
        return np.stack([np.asarray(r[i][name], np.float32).reshape(shape) for i in range(n)], axis=0)

    outs = (
        stack("y_p", (SEQ, D)), stack("y_s", (DEC, D)),
        stack("conv_p", (CS, CONV_DIM))[None],
        stack("k_p", (SEQ, SBH, 128))[None], stack("v_p", (SEQ, SBH, 128))[None],
        stack("mk_p", (MEM, XH, 256))[None], stack("mv_p", (MEM, XH, 256))[None],
        stack("conv_s", (CS, CONV_DIM))[None],
        stack("k_s", (DEC, SBH, 128))[None], stack("v_s", (DEC, SBH, 128))[None],
    )
    return outs, res


def kernel(**inputs):
    outs, _ = run(inputs, 2048, 2048, 5632, 512)
    return outs
```
